# Optimizing a Trainium2 kernel written in Bass

```python
import math
import jax, jax.numpy as jnp
from jax import lax
import numpy as np

D_MODEL = 1024
BATCH = 8
SEQ = 4096
DEPTH = 1

CTX_LEN = 256
GRID_W = 64
D_CONV = D_MODEL
CONV_WIDTH = 3
GLA_HEADS = 4
GLA_DK = D_MODEL // (2 * GLA_HEADS)
GLA_DV = D_MODEL // GLA_HEADS
GLA_RANK = 16
GLA_TAU = 16.0
GLA_CHUNK = 64
QK_DIM = GLA_HEADS * GLA_DK
V_DIM = GLA_HEADS * GLA_DV
D_FF = 4 * D_MODEL
N_MOD = 6
RMS_EPS = 1e-6
IN_SPLITS = (D_CONV, D_CONV, D_CONV, QK_DIM, QK_DIM, V_DIM, V_DIM, GLA_RANK, GLA_RANK, D_MODEL, D_MODEL)
N_IN = sum(IN_SPLITS)

kernel_name = "hybrid_shortconv_gla_prefix_dit_block"


def rmsnorm(x, g):
    xf = x.astype(jnp.float32)
    y = xf * lax.rsqrt(jnp.mean(xf * xf, axis=-1, keepdims=True) + RMS_EPS)
    return (y * g.astype(jnp.float32)).astype(x.dtype)


def modulate(x, g, shift, scale):
    return rmsnorm(x, g) * (1.0 + scale) + shift


def in_proj(h, w, b):
    z = h @ w + b
    idx = []
    acc = 0
    for s in IN_SPLITS[:-1]:
        acc += s
        idx.append(acc)
    return jnp.split(z, idx, axis=-1)


def split_heads(a, d):
    b, t, _ = a.shape
    return a.reshape(b, t, GLA_HEADS, d).transpose(0, 2, 1, 3)


def conv3_grid(u, w):
    b, l, ch = u.shape
    rows = l // GRID_W
    g = jnp.pad(u.reshape(b, rows, GRID_W, ch), ((0, 0), (0, 0), (1, 1), (0, 0)))
    y = g[:, :, :-2] * w[0] + g[:, :, 1:-1] * w[1] + g[:, :, 2:] * w[2]
    return y.reshape(b, l, ch)


def conv3_seq(u, w):
    g = jnp.pad(u, ((0, 0), (1, 1), (0, 0)))
    return g[:, :-2] * w[0] + g[:, 1:-1] * w[1] + g[:, 2:] * w[2]


def gla_kvg(k, v, lr_f, lr_b, w_a2_f, b_a_f, w_a2_b, b_a_b):
    k = split_heads(k, GLA_DK)
    v = split_heads(v, GLA_DV)
    g_f = split_heads(jax.nn.log_sigmoid((lr_f @ w_a2_f + b_a_f).astype(jnp.float32)) / GLA_TAU, GLA_DK)
    g_b = split_heads(jax.nn.log_sigmoid((lr_b @ w_a2_b + b_a_b).astype(jnp.float32)) / GLA_TAU, GLA_DK)
    return k, v, g_f, g_b


def gla_final_state(k, v, g):
    gcum = jnp.cumsum(g, axis=2)
    wdec = jnp.exp(gcum[:, :, -1:, :] - gcum)
    return jnp.einsum('bhtd,bhtv->bhdv', k.astype(jnp.float32) * wdec, v.astype(jnp.float32))


def gla_chunk_scan(q, k, v, g, s0):
    b, h, t, dk = q.shape
    dv = v.shape[-1]
    n = t // GLA_CHUNK

    def to_chunks(a):
        return jnp.moveaxis(a.reshape(b, h, n, GLA_CHUNK, a.shape[-1]), 2, 0)

    mask = jnp.tril(jnp.ones((GLA_CHUNK, GLA_CHUNK), dtype=bool))[:, :, None]

    def step(s, inp):
        qc, kc, vc, gc = inp
        qc = qc.astype(jnp.float32)
        kc = kc.astype(jnp.float32)
        vc = vc.astype(jnp.float32)
        gcum = jnp.cumsum(gc, axis=2)
        diff = gcum[:, :, :, None, :] - gcum[:, :, None, :, :]
        decay = jnp.exp(jnp.where(mask, diff, -jnp.inf))
        scores = jnp.einsum('bhtd,bhtsd,bhsd->bhts', qc, decay, kc)
        o = (jnp.einsum('bhts,bhsv->bhtv', scores, vc)
             + jnp.einsum('bhtd,bhdv->bhtv', qc * jnp.exp(gcum), s))
        g_last = gcum[:, :, -1, :]
        s_new = (jnp.exp(g_last)[..., None] * s
                 + jnp.einsum('bhsd,bhsv->bhdv', kc * jnp.exp(g_last[:, :, None, :] - gcum), vc))
        return s_new, o

    _, o = lax.scan(step, s0.astype(jnp.float32), (to_chunks(q), to_chunks(k), to_chunks(v), to_chunks(g)))
    return jnp.moveaxis(o, 0, 2).reshape(b, h, t, dv)


def gla_bidir(q, k, v, g_f, g_b, s_f, s_b):
    def flip(a):
        return jnp.flip(a, axis=2)
    o_f = gla_chunk_scan(q, k, v, g_f, s_f)
    o_b = gla_chunk_scan(flip(q), flip(k), flip(v), flip(g_b), s_b)
    return o_f + flip(o_b)


def mixer_merge(a_x, a_b, a_c, r, gate_a, gate_b, o_gla, conv_w, w_conv_out, g_gla_norm, w_gla_out, conv_fn):
    y_a = (a_b * conv_fn(a_c * a_x, conv_w)) @ w_conv_out
    b, h, t, dv = o_gla.shape
    o = rmsnorm(o_gla.transpose(0, 2, 1, 3), g_gla_norm.reshape(GLA_HEADS, GLA_DV)).reshape(b, t, h * dv)
    y_b = (o.astype(r.dtype) * jax.nn.silu(r)) @ w_gla_out
    return jax.nn.sigmoid(gate_a) * y_a + jax.nn.sigmoid(gate_b) * y_b


def sq_relu_mlp(h, w_up, w_down):
    return jnp.square(jax.nn.relu(h @ w_up)) @ w_down


def setup_inputs(seed: int = 0) -> dict:
    key = jax.random.key(seed)
    ks = jax.random.split(key, 24)
    nrm = jax.random.normal
    f32 = jnp.float32
    d = D_MODEL
    return {
        "x": nrm(ks[0], (BATCH, SEQ, d), f32),
        "c": nrm(ks[1], (BATCH, d), f32),
        "ctx": nrm(ks[2], (BATCH, CTX_LEN, d), f32),
        "c_ctx": nrm(ks[3], (d,), f32),
        "w_ada": nrm(ks[4], (DEPTH, d, N_MOD * d), f32) * (0.5 * d ** -0.5),
        "b_ada": nrm(ks[5], (DEPTH, N_MOD * d), f32) * 0.01,
        "g_norm1": 1.0 + 0.01 * nrm(ks[6], (DEPTH, d), f32),
        "w_in": nrm(ks[7], (DEPTH, d, N_IN), f32) * d ** -0.5,
        "b_in": nrm(ks[8], (DEPTH, N_IN), f32) * 0.01,
        "conv_w": nrm(ks[9], (DEPTH, CONV_WIDTH, D_CONV), f32) * CONV_WIDTH ** -0.5,
        "w_conv_out": nrm(ks[10], (DEPTH, D_CONV, d), f32) * D_CONV ** -0.5,
        "w_a2_f": nrm(ks[11], (DEPTH, GLA_RANK, QK_DIM), f32) * GLA_RANK ** -0.5,
        "b_a_f": 1.0 + 0.1 * nrm(ks[12], (DEPTH, QK_DIM), f32),
        "w_a2_b": nrm(ks[13], (DEPTH, GLA_RANK, QK_DIM), f32) * GLA_RANK ** -0.5,
        "b_a_b": 1.0 + 0.1 * nrm(ks[14], (DEPTH, QK_DIM), f32),
        "g_gla_norm": 1.0 + 0.01 * nrm(ks[15], (DEPTH, V_DIM), f32),
        "w_gla_out": nrm(ks[16], (DEPTH, V_DIM, d), f32) * V_DIM ** -0.5,
        "w_o": nrm(ks[17], (DEPTH, d, d), f32) * d ** -0.5,
        "g_norm2": 1.0 + 0.01 * nrm(ks[18], (DEPTH, d), f32),
        "w_up": nrm(ks[19], (DEPTH, d, D_FF), f32) * d ** -0.5,
        "w_down": nrm(ks[20], (DEPTH, D_FF, d), f32) * D_FF ** -0.5,
        "g_final": 1.0 + 0.01 * nrm(ks[21], (d,), f32),
    }


def reference(x, c, ctx, c_ctx, w_ada, b_ada, g_norm1, w_in, b_in, conv_w, w_conv_out, w_a2_f, b_a_f,
              w_a2_b, b_a_b, g_gla_norm, w_gla_out, w_o, g_norm2, w_up, w_down, g_final):
    ctx_s = ctx
    for i in range(DEPTH):
        mod_lat = (jax.nn.silu(c) @ w_ada[i] + b_ada[i])[:, None, :]
        mod_ctx = jax.nn.silu(c_ctx) @ w_ada[i] + b_ada[i]
        sh1, sc1, ga1, sh2, sc2, ga2 = jnp.split(mod_lat, N_MOD, axis=-1)
        cmod = jnp.split(mod_ctx, N_MOD, axis=-1)

        (cx_a, cb_a, cc_a, cq, ck, cv, cr, clr_f, clr_b, cgate_a, cgate_b) = in_proj(
            modulate(ctx_s, g_norm1[i], cmod[0], cmod[1]), w_in[i], b_in[i])
        kc, vc, gfc, gbc = gla_kvg(ck, cv, clr_f, clr_b, w_a2_f[i], b_a_f[i], w_a2_b[i], b_a_b[i])
        s_f = gla_final_state(kc, vc, gfc)
        s_b = gla_final_state(jnp.flip(kc, 2), jnp.flip(vc, 2), jnp.flip(gbc, 2))

        (x_a, b_a, c_a, q, k, v, r, lr_f, lr_b, gate_a, gate_b) = in_proj(
            modulate(x, g_norm1[i], sh1, sc1), w_in[i], b_in[i])
        kl, vl, gfl, gbl = gla_kvg(k, v, lr_f, lr_b, w_a2_f[i], b_a_f[i], w_a2_b[i], b_a_b[i])
        ql = split_heads(q, GLA_DK) * GLA_DK ** -0.5
        o_lat = gla_bidir(ql, kl, vl, gfl, gbl, s_f, s_b)
        y_lat = mixer_merge(x_a, b_a, c_a, r, gate_a, gate_b, o_lat, conv_w[i], w_conv_out[i],
                            g_gla_norm[i], w_gla_out[i], conv3_grid)
        x = x + ga1 * (y_lat @ w_o[i])
        x = x + ga2 * sq_relu_mlp(modulate(x, g_norm2[i], sh2, sc2), w_up[i], w_down[i])

        if i < DEPTH - 1:
            qc = split_heads(cq, GLA_DK) * GLA_DK ** -0.5
            zeros = jnp.zeros_like(s_f)
            o_ctx = gla_bidir(qc, kc, vc, gfc, gbc, zeros, zeros)
            y_ctx = mixer_merge(cx_a, cb_a, cc_a, cr, cgate_a, cgate_b, o_ctx, conv_w[i], w_conv_out[i],
                                g_gla_norm[i], w_gla_out[i], conv3_seq)
            ctx_s = ctx_s + cmod[2] * (y_ctx @ w_o[i])
            ctx_s = ctx_s + cmod[5] * sq_relu_mlp(modulate(ctx_s, g_norm2[i], cmod[3], cmod[4]),
                                                  w_up[i], w_down[i])
    return rmsnorm(x, g_final)
```

```python
import numpy as np
import concourse.bass as bass
import concourse.mybir as mybir
from concourse.bass_utils import run_bass_kernel_spmd
from contextlib import ExitStack

F32 = mybir.dt.float32
BF = mybir.dt.bfloat16
AF = mybir.ActivationFunctionType
ALU = mybir.AluOpType
AX = mybir.AxisListType

D = 1024
SEQ = 4096
CTX = 256
TT = 512
NT = SEQ // TT
NFM = 105
NTM = 13
EPS = 1e-6
QSCALE = 128.0 ** -0.5

CELL = 256
SEM_CAP = 20000


def ap_cells(ap):
    name = ap.tensor.name
    if name.startswith("ps"):
        return {(name, 0)}
    es = mybir.dt.size(ap.dtype)
    pat = ap.ap
    pstep = pat[0][0]
    off = int(ap.offset)
    base = (off % pstep) * es if pstep else off * es
    dims = [(s, c) for (s, c) in pat[1:] if c > 1]
    if not dims:
        dims = [(1, 1)]
    *outer, (ls, lc) = dims
    if ls == 1:
        run = lc * es
    elif ls == 0:
        run = es
    else:
        run = ((lc - 1) * abs(ls) + 1) * es
    n = 1
    for (s, c) in outer:
        n *= c
    if n > 256:
        lo = base
        hi = base + run
        for (s, c) in outer:
            hi += (c - 1) * abs(s) * es
        return {(name, k) for k in range(lo // CELL, (hi - 1) // CELL + 1)}
    starts = [base]
    for (s, c) in outer:
        starts = [b + i * s * es for b in starts for i in range(c)]
    cells = set()
    for b in starts:
        for k in range(b // CELL, (b + run - 1) // CELL + 1):
            cells.add((name, k))
    return cells


class Op:
    __slots__ = ("idx", "eng", "fn", "wcells", "rcells", "dma", "dkey", "waits",
                 "sig", "sigval", "dcount")

    def __init__(self):
        self.waits = []
        self.sig = False
        self.sigval = None
        self.dcount = None


class Prog:
    def __init__(self, nc):
        self.nc = nc
        self.ops = []

    def add(self, eng, fn, outs, ins, dma=False, dkey=None):
        op = Op()
        op.idx = len(self.ops)
        op.eng = eng
        op.fn = fn
        op.dma = dma
        op.dkey = dkey
        w = set()
        r = set()
        for a in outs:
            w |= ap_cells(a)
        for a in ins:
            r |= ap_cells(a)
        op.wcells = w
        op.rcells = r
        self.ops.append(op)
        return op

    def resolve(self):
        last_w = {}
        readers = {}
        known = {}
        dcounts = {}
        for op in self.ops:
            deps = {}
            for c in op.rcells:
                j = last_w.get(c)
                if j is not None:
                    deps[j] = "raw"
                if c[0].startswith("ps"):
                    rd = readers.get(c)
                    if rd:
                        for e2, j2 in rd.items():
                            if e2 != op.eng and j2 not in deps:
                                deps[j2] = "rar"
            for c in op.wcells:
                j = last_w.get(c)
                if j is not None and j not in deps:
                    deps[j] = "waw"
                rd = readers.get(c)
                if rd:
                    for j in rd.values():
                        if j not in deps:
                            deps[j] = "war"
            for c in op.rcells:
                if c in op.wcells:
                    continue
                d = readers.get(c)
                if d is None:
                    d = readers[c] = {}
                d[("dma", op.idx) if op.dma else op.eng] = op.idx
            for c in op.wcells:
                last_w[c] = op.idx
                readers[c] = {}
            if op.dma:
                dcounts[op.dkey] = dcounts.get(op.dkey, 0) + 1
                op.dcount = dcounts[op.dkey]
            kn = known.setdefault(op.eng, {})
            for j, kind in deps.items():
                p = self.ops[j]
                if p.dma:
                    key = ("dma", p.dkey)
                    if kn.get(key, 0) >= p.dcount:
                        continue
                    kn[key] = p.dcount
                    op.waits.append(("dma", p.dkey, p.dcount))
                else:
                    if (not op.dma) and p.eng == op.eng and op.eng == "pe":
                        continue
                    key = ("eng", p.eng)
                    if kn.get(key, -1) >= j:
                        continue
                    kn[key] = j
                    p.sig = True
                    op.waits.append(("eng", j))
        self.dtotals = dcounts
        cnt = {}
        for op in self.ops:
            if op.sig and not op.dma:
                k = cnt.get(op.eng, 0)
                op.sigval = (k // SEM_CAP, k % SEM_CAP + 1)
                cnt[op.eng] = k + 1
        self.sigcounts = cnt

    def emit(self, final_waits=()):
        nc = self.nc
        self.resolve()
        with ExitStack() as es:
            esem = {}
            for e, k in self.sigcounts.items():
                esem[e] = [es.enter_context(nc.semaphore(f"s_{e}{i}"))
                           for i in range((k - 1) // SEM_CAP + 1)]
            dsem = {key: es.enter_context(nc.semaphore(f"d_{key}")) for key in self.dtotals}
            block = es.enter_context(nc.Block())
            queues = {"pe": block.tensor, "act": block.scalar, "dve": block.vector,
                      "pool": block.gpsimd, "sp": block.sync}
            by_eng = {q: [] for q in queues}
            for op in self.ops:
                by_eng[op.eng].append(op)

            def make(qname):
                ops = by_eng[qname]

                def body(eng):
                    for op in ops:
                        for w in op.waits:
                            if w[0] == "eng":
                                p = self.ops[w[1]]
                                si, v = p.sigval
                                eng.wait_ge(esem[p.eng][si], v)
                            else:
                                eng.wait_ge(dsem[w[1]], 16 * w[2])
                        ins = op.fn(eng)
                        if op.dma:
                            ins.then_inc(dsem[op.dkey], 16)
                        elif op.sig:
                            si, v = op.sigval
                            ins.then_inc(esem[op.eng][si], 1)
                    if qname == "sp":
                        for key in final_waits:
                            eng.wait_ge(dsem[key], 16 * self.dtotals[key])
                return body

            for qname, deco in queues.items():
                if by_eng[qname] or qname == "sp":
                    deco(make(qname))


def build_nc():
    nc = bass.Bass("TRN2", target_bir_lowering=False)

    def din(name, shape):
        return nc.dram_tensor(name, list(shape), F32, kind="ExternalInput").ap()

    x_d = din("x", [SEQ, D])
    ctx_d = din("ctx", [CTX, D])
    cc_d = din("cc", [128, 16])
    wada_d = din("wada", [12, 128, 8 * 512])
    badarow_d = din("badarow", [1, 6144])
    smallpm_d = din("smallpm", [128, 48 + 57 + 8 + 8 + 8 + 24])
    binkv_d = din("binkv", [1, 1536])
    wfm_d = din("wfm", [NFM, 128, 1024])
    wtm_d = din("wtm", [NTM, 128, 4096])
    wa2_d = din("wa2", [2, 128, 512])
    gfin_d = din("gfin", [128, 1024])
    cst_d = din("cst", [128, 8 * 128])
    out_d = nc.dram_tensor("out", [SEQ, D], F32, kind="ExternalOutput").ap()

    es = ExitStack()

    def sb(name, shape, dt):
        return es.enter_context(nc.sbuf_tensor("sb_" + name, list(shape), dt))

    xres = sb("xres", [128, 4, 1024], F32)
    xst = sb("xst", [128, 1024], F32)
    hT = sb("hT", [128, 8, 512], BF)
    ogT = sb("ogT", [128, 8, 512], BF)
    arena = sb("arena", [128, 16384], BF)
    xn = sb("xn", [128, 4, 1024], BF)
    scr = sb("scr", [128, 6144], F32)
    scrb = sb("scrb", [128, 3072], BF)
    Sf = sb("Sf", [128, 4, 256], F32)
    Sb = sb("Sb", [128, 4, 256], F32)
    Sfbf = sb("Sfbf", [128, 4, 256], BF)
    Sbbf = sb("Sbbf", [128, 4, 1024], BF)
    Bnd = sb("Bnd", [128, 8, 1024], BF)
    ringA = sb("ringA", [128, 8, 1024], BF)
    ringB = sb("ringB", [128, 4, 2048], BF)
    cst = sb("cst", [128, 8, 128], F32)
    identb = sb("identb", [128, 128], BF)
    maskb = sb("maskb", [128, 2, 128], BF)
    binkv_bc = sb("binkv_bc", [128, 1536], F32)
    ga_bc = sb("ga_bc", [128, 2, 1024], F32)
    gfin_bc = sb("gfin_bc", [128, 1024], F32)
    wa2 = sb("wa2", [128, 2, 512], BF)
    lrT = sb("lrT", [128, 512], BF)
    tri_bf = sb("tri_bf", [128, 4, 128], BF)
    smallpm = sb("smallpm", [128, 153], F32)
    cc = sb("cc", [128, 16], F32)
    sil = sb("sil", [128, 16], F32)
    modpm = sb("modpm", [128, 6 * 8 * 2], F32)
    gmod = sb("gmod", [128, 3, 8], F32)
    gg16 = sb("gg16", [128, 8], F32)
    stat = sb("stat", [128, 48], F32)
    decs = sb("decs", [128, 32], F32)
    epsc = sb("epsc", [128, 1], F32)
    bq_s = sb("bq_s", [128, 4], F32)

    ps = [es.enter_context(nc.psum_tensor(f"ps{i}", [128, 512], F32)) for i in range(8)]

    qT = arena[:, 0:2048].rearrange("p (h t) -> p h t", h=4)
    kT = arena[:, 2048:4096].rearrange("p (h t) -> p h t", h=4)
    ktm = arena[:, 4096:6144].rearrange("p (s d) -> p s d", s=4)
    vtm = arena[:, 6144:10240].rearrange("p (s d) -> p s d", s=4)
    qbB = arena[:, 10240:12288].rearrange("p (c t) -> p c t", c=4)
    kbB = arena[:, 12288:14336].rearrange("p (c t) -> p c t", c=4)
    kdecB = arena[:, 14336:16384].rearrange("p (c t) -> p c t", c=4)
    yT = arena[:, 4096:8192].rearrange("p (k t) -> p k t", k=8)
    aT = arena[:, :].rearrange("p (k t) -> p k t", k=32)
    arena_f = arena[:, :].bitcast(F32)
    wstage = arena_f.rearrange("p (s k c) -> p s k c", s=2, k=8)
    ypre = xn[:, :, :].rearrange("p s f -> p (s f)").rearrange("p (k t) -> p k t", k=8)
    xn_f = xn[:, :, :].rearrange("p s f -> p (s f)").bitcast(F32).rearrange("p (o f) -> p o f", o=2)

    def S(i):
        return scr[:, i * 512:(i + 1) * 512]
    XA = S(8)
    ACC = S(9)
    sq = scr[:, 5120:6144]
    b_p = [[scrb[:, 0:512], scrb[:, 512:1024]], [scrb[:, 1024:1536], scrb[:, 1536:2048]]]
    b_on = scrb[:, 2048:3072]
    sil_rep = scrb[:, 0:1024].rearrange("p (k m) -> p k m", k=8)

    bada_pm = smallpm[:, 0:48]
    bin_pm = smallpm[:, 48:105]
    g1_pm = smallpm[:, 105:113]
    g2_pm = smallpm[:, 113:121]
    gg_pm = smallpm[:, 121:129]
    cw_pm = smallpm[:, 129:153].rearrange("p (j i) -> p j i", i=3)
    modv = modpm[:, :].rearrange("p (g j l) -> p g j l", g=6, j=8)
    sil3 = sil[:, :].rearrange("p (k l) -> p k l", l=2)
    TRI_U, TRI_L, TRI_SL, TRI_SU, C_U01, C_L01, C_ID, C_ONE = range(8)

    P = Prog(nc)
    state = {"bank": 0, "ra": 0, "rb": 0, "ost": 0, "dset": 0}

    state["banks"] = list(range(8))

    def nb():
        bl = state["banks"]
        b = bl[state["bank"] % len(bl)]
        state["bank"] += 1
        return ps[b]

    def dma(q, out, in_, key, track_in=False):
        P.add(q, lambda e: e.dma_start(out=out, in_=in_), [out] if out.tensor.name != "out" else [],
              [in_] if track_in else [], dma=True, dkey=key)

    def mm(out, lhsT, rhs, start, stop):
        P.add("pe", lambda e: e.matmul(out, lhsT, rhs, start=start, stop=stop), [out], [lhsT, rhs])

    def tr(out, in_):
        P.add("pe", lambda e: e.transpose(out, in_, identb[:, :]), [out], [in_, identb[:, :]])

    def act(out, in_, func, bias=None, scale=None):
        ins = [in_]
        kw = {}
        if bias is not None:
            kw["bias"] = bias
            if not isinstance(bias, float):
                ins.append(bias)
        if scale is not None:
            kw["scale"] = scale
            if not isinstance(scale, float):
                ins.append(scale)
        P.add("act", lambda e: e.activation(out=out, in_=in_, func=func, **kw), [out], ins)

    def tt(out, in0, in1, op, eng="dve"):
        P.add(eng, lambda e: e.tensor_tensor(out=out, in0=in0, in1=in1, op=op), [out], [in0, in1])

    def ts(out, in0, s1, op0, s2=None, op1=None, eng="dve"):
        ins = [in0] + [s for s in (s1, s2) if s is not None and not isinstance(s, float)]
        if op1 is None:
            P.add(eng, lambda e: e.tensor_scalar(out=out, in0=in0, scalar1=s1, scalar2=None, op0=op0),
                  [out], ins)
        else:
            P.add(eng, lambda e: e.tensor_scalar(out=out, in0=in0, scalar1=s1, scalar2=s2, op0=op0, op1=op1),
                  [out], ins)

    def stt(out, in0, scalar, in1, op0, op1):
        ins = [in0, in1] + ([] if isinstance(scalar, float) else [scalar])
        P.add("dve", lambda e: e.scalar_tensor_tensor(out=out, in0=in0, scalar=scalar, in1=in1, op0=op0, op1=op1),
              [out], ins)

    def copy(eng, out, in_):
        if eng == "act":
            P.add("act", lambda e: e.copy(out=out, in_=in_), [out], [in_])
        else:
            P.add(eng, lambda e: e.tensor_copy(out=out, in_=in_), [out], [in_])

    def memset(out, val, eng="dve"):
        P.add(eng, lambda e: e.memset(out, val), [out], [])

    def loadA(idx):
        s = state["ra"]
        state["ra"] = (s + 1) % 8
        dma("pool", ringA[:, s, :], wfm_d[idx], f"ra{s}")
        return ringA[:, s, :].rearrange("p (k c) -> p k c", k=8)

    def loadB(idx):
        halves = []
        for hf in range(2):
            s = state["rb"]
            state["rb"] = (s + 1) % 4
            dma("pool", ringB[:, s, :], wtm_d[idx][:, hf * 2048:(hf + 1) * 2048], f"rb{s}")
            halves.append(ringB[:, s, :].rearrange("p (k c) -> p k c", k=4))
        return lambda kc: halves[kc // 4][:, kc % 4, :]

    def fm_group(panel, rhs_fn, ntok=512):
        bank = nb()
        for kc in range(8):
            mm(bank[:, 0:ntok], panel[:, kc, :], rhs_fn(kc), kc == 0, kc == 7)
        return bank

    dma("sp", cst[:, :, :], cst_d.rearrange("p (a b) -> p a b", a=8), "setup0")
    dma("sp", smallpm[:, :], smallpm_d, "setup1")
    dma("sp", cc[:, :], cc_d, "setup2")
    dma("sp", ga_bc[:, 0, :], badarow_d[0:1, 2048:3072].broadcast_to([128, 1024]), "setup3")
    dma("sp", ga_bc[:, 1, :], badarow_d[0:1, 5120:6144].broadcast_to([128, 1024]), "setup3b")
    dma("sp", binkv_bc[:, :], binkv_d[0:1, :].broadcast_to([128, 1536]), "setup4")
    dma("pool", wa2[:, :, :], wa2_d.rearrange("a p c -> p a c"), "setup5")
    dma("sp", gfin_bc[:, :], gfin_d, "setup6")

    copy("dve", identb[:, :], cst[:, C_ID, :])
    copy("dve", tri_bf[:, :, :], cst[:, 0:4, :])
    copy("dve", maskb[:, 0, :], cst[:, C_U01, :])
    copy("dve", maskb[:, 1, :], cst[:, C_L01, :])
    ones = cst[:, C_ONE, :]
    act(sil[:, :], cc[:, :], AF.Silu)
    for kc in range(8):
        ts(sil_rep[:, kc, :], ones, sil3[:, kc, 0:1], ALU.mult)
    ts(gg16[:, :], gg_pm, 1.0, ALU.mult)
    memset(epsc[:, :], EPS)
    ts(bq_s[:, :], bin_pm[:, 0:4], QSCALE, ALU.mult)
    memset(Sf[:, :, :], 0.0)
    memset(Sb[:, :, :], 0.0)

    sil_bf = sb("sil_bf", [128, 16], BF)
    copy("dve", sil_bf[:, :], sil[:, :])
    silb3 = sil_bf[:, :].rearrange("p (k l) -> p k l", l=2)
    ada_stage = [arena[:, 0:4096].rearrange("p (k c) -> p k c", k=8),
                 arena[:, 4096:8192].rearrange("p (k c) -> p k c", k=8),
                 Sbbf[:, :, :].rearrange("p a b -> p (a b)").rearrange("p (k c) -> p k c", k=8)]

    def ada_piece(j, slot):
        g = j // 2
        st = ada_stage[slot]
        dma("pool", st, wada_d[j].rearrange("p (k c) -> p k c", k=8), f"wst{slot}")
        if g in (2, 5):
            bank = nb()
            for kc in range(8):
                mm(bank[:, :], sil_rep[:, kc, :], st[:, kc, :], kc == 0, kc == 7)
            gi = 0 if g == 2 else 1
            dst = ga_bc[:, gi, (j % 2) * 512:(j % 2 + 1) * 512]
            tt(dst, bank[:, :], dst, ALU.add)
        else:
            bank = nb()
            for q in range(4):
                for kc in range(8):
                    mm(bank[:, q * 2:q * 2 + 2], st[:, kc, q * 128:(q + 1) * 128], silb3[:, kc, :],
                       kc == 0, kc == 7)
            for q in range(4):
                ch = (j % 2) * 4 + q
                ts(modv[:, g, ch, :], bank[:, q * 2:q * 2 + 2], bada_pm[:, g * 8 + ch:g * 8 + ch + 1], ALU.add)

    for j in range(4):
        ada_piece(j, j % 2)
    stt(gmod[:, 0, :], modv[:, 1, :, 0], 1.0, g1_pm, ALU.add, ALU.mult)
    stt(gmod[:, 1, :], modv[:, 1, :, 1], 1.0, g1_pm, ALU.add, ALU.mult)
    ada_defer = [(lambda j=j: ada_piece(j, 2)) for j in range(4, 12)]

    def norm_stats(src, s, sc, on_act=False):
        act(sq, src, AF.Square)
        P.add("dve", lambda e: e.tensor_reduce(out=stat[:, sc:sc + 1], in_=sq, axis=AX.X, op=ALU.add),
              [stat[:, sc:sc + 1]], [sq])
        act(stat[:, sc + 1:sc + 2], stat[:, sc:sc + 1], AF.Ln, bias=epsc[:, 0:1], scale=1.0 / D)
        act(stat[:, sc + 2:sc + 3], stat[:, sc + 1:sc + 2], AF.Exp, scale=-0.5)
        if on_act:
            act(xn[:, s, :], src, AF.Copy, scale=stat[:, sc + 2:sc + 3])
        else:
            ts(xn[:, s, :], src, stat[:, sc + 2:sc + 3], ALU.mult)

    def norm_T(nsub, gm_idx, sh_g, sh_l, js=range(8), hTv=None, alt=False):
        hTv = hT if hTv is None else hTv
        nt = nsub * 128
        for j in js:
            bank = nb()
            bv = bank[:, :].bitcast(BF)
            for s in range(nsub):
                tr(bv[:, s * 128:(s + 1) * 128], xn[:, s, j * 128:(j + 1) * 128])
            if alt and j % 2 == 1:
                ts(hTv[:, j, 0:nt], bv[:, 0:nt], gmod[:, gm_idx, j:j + 1], ALU.mult,
                   modv[:, sh_g, j, sh_l:sh_l + 1], ALU.add)
            else:
                act(hTv[:, j, 0:nt], bv[:, 0:nt], AF.Identity, bias=modv[:, sh_g, j, sh_l:sh_l + 1],
                    scale=gmod[:, gm_idx, j:j + 1])

    MAINB = dict(ktm=ktm, vtm=vtm, kdecB=kdecB, lrT=lrT)

    def kv_units(nsub, B=None, hTv=None, pieces=(0, 1, 2)):
        B = MAINB if B is None else B
        hTv = hT if hTv is None else hTv
        units = []
        holder = {}

        def unit(piece, s):
            def f():
                if s == 0:
                    holder[piece] = loadB(piece)
                w = holder[piece]
                bank = nb()
                for kc in range(8):
                    mm(bank[:, :], hTv[:, kc, s * 128:(s + 1) * 128], w(kc), kc == 0, kc == 7)
                if piece == 0:
                    tt(B["ktm"][:, s, :], bank[:, :], binkv_bc[:, 0:512], ALU.add)
                else:
                    tt(B["vtm"][:, s, (piece - 1) * 512:piece * 512], bank[:, :],
                       binkv_bc[:, piece * 512:(piece + 1) * 512], ALU.add)
            return f
        for piece in pieces:
            for s in range(nsub):
                units.append(unit(piece, s))
        return units

    def inproj_kv(nsub, B=None, hTv=None, pieces=(0, 1, 2)):
        for u in kv_units(nsub, B, hTv, pieces):
            u()

    def norm_T_pipelined(gm_idx, sh_g, sh_l):
        banks = [nb() for _ in range(4)]
        bvs = [b[:, :].bitcast(BF) for b in banks]
        for s in range(4):
            for j in range(8):
                tr(bvs[j // 2][:, (j % 2) * 512 + s * 128:(j % 2) * 512 + (s + 1) * 128],
                   xn[:, s, j * 128:(j + 1) * 128])
        for j in range(8):
            src = bvs[j // 2][:, (j % 2) * 512:(j % 2 + 1) * 512]
            if (j // 2) % 2 == 0:
                act(hT[:, j, :], src, AF.Identity, bias=modv[:, sh_g, j, sh_l:sh_l + 1],
                    scale=gmod[:, gm_idx, j:j + 1])
            else:
                ts(hT[:, j, :], src, gmod[:, gm_idx, j:j + 1], ALU.mult, modv[:, sh_g, j, sh_l:sh_l + 1], ALU.add)

    def inproj_lr(nt, B=None, hTv=None):
        B = MAINB if B is None else B
        hTv = hT if hTv is None else hTv
        w = loadA(8)
        bank = fm_group(w, lambda kc: hTv[:, kc, 0:nt], nt)
        act(B["lrT"][:, 0:nt], bank[:, 0:nt], AF.Identity, bias=bin_pm[:, 8:9])

    def v4(ap512):
        return ap512.rearrange("p (h t) -> p h t", h=4)

    def gla_Da(c, d, B=None):
        B = MAINB if B is None else B
        ds_ = state["dset"]
        state["dset"] = 1 - ds_
        SP = S(4 * ds_)
        SPh = S(4 * ds_ + 1).bitcast(BF)[:, 512:1024]
        SPl = S(4 * ds_ + 2).bitcast(BF)[:, 512:1024]
        t0 = c * 128
        bank = nb()
        mm(bank[:, :], B["lrT"][:, t0:t0 + 128], wa2[:, d, :], True, True)
        act(SP, bank[:, :], AF.Exp, scale=-1.0)
        act(SPh, SP, AF.Ln, bias=1.0)
        return (c, d, ds_)

    def gla_Db(tok, with_qk, B=None):
        B = MAINB if B is None else B
        c, d, ds_ = tok
        ktm_, kdecB_ = B["ktm"], B["kdecB"]
        SP = S(4 * ds_)
        EREM = S(4 * ds_ + 1).bitcast(BF)[:, 0:512]
        EB = S(4 * ds_ + 2).bitcast(BF)[:, 0:512]
        ENB = S(4 * ds_ + 3).bitcast(BF)[:, 0:512]
        SPh = S(4 * ds_ + 1).bitcast(BF)[:, 512:1024]
        SPl = S(4 * ds_ + 2).bitcast(BF)[:, 512:1024]
        t0 = c * 128
        triS = tri_bf[:, TRI_SL if d == 0 else TRI_SU, :]
        triI = tri_bf[:, TRI_U if d == 0 else TRI_L, :]
        bank2 = nb()
        mm(bank2[:, :], triS, SPh, True, True)
        bank3 = nb()
        for h in range(4):
            mm(bank3[:, h * 128:(h + 1) * 128], SPh[:, h * 128:(h + 1) * 128], triI, True, True)
        act(EREM, bank2[:, :], AF.Exp)
        col = 127 if d == 0 else 0
        act(decs[:, (c * 2 + d) * 4:(c * 2 + d) * 4 + 4], v4(bank3[:, :])[:, :, col], AF.Exp)
        if d == 1:
            tt(kdecB_[:, c, :], ktm_[:, c, :], EREM, ALU.mult)
        else:
            tt(ktm_[:, c, :], ktm_[:, c, :], EREM, ALU.mult)
        if with_qk:
            act(EB, bank3[:, :], AF.Exp)
            act(ENB, bank3[:, :], AF.Exp, scale=-1.0)
            if d == 1:
                tt(v4(qbB[:, c, :]), qT[:, :, t0:t0 + 128], v4(EB), ALU.mult)
                tt(v4(kbB[:, c, :]), kT[:, :, t0:t0 + 128], v4(ENB), ALU.mult)
            else:
                tt(qT[:, :, t0:t0 + 128], qT[:, :, t0:t0 + 128], v4(EB), ALU.mult)
                tt(kT[:, :, t0:t0 + 128], kT[:, :, t0:t0 + 128], v4(ENB), ALU.mult)

    def gla_D(c, d, with_qk, B=None):
        gla_Db(gla_Da(c, d, B), with_qk, B)

    def state_update(c, d, Sst, B=None):
        B = MAINB if B is None else B
        kd = B["kdecB"][:, c, :] if d == 1 else B["ktm"][:, c, :]
        banks = (nb(), nb())
        for h in range(4):
            mm(banks[h // 2][:, (h % 2) * 256:(h % 2 + 1) * 256], kd[:, h * 128:(h + 1) * 128],
               B["vtm"][:, c, h * 256:(h + 1) * 256], True, True)
        for h in range(4):
            i0 = (c * 2 + d) * 4 + h
            stt(Sst[:, h, :], Sst[:, h, :], decs[:, i0:i0 + 1],
                banks[h // 2][:, (h % 2) * 256:(h % 2 + 1) * 256], ALU.mult, ALU.add)

    def xload(dst, i, s, key):
        dma("sp", dst, x_d[i * TT + s * 128:i * TT + (s + 1) * 128, :], key)

    P1B = []
    for b in range(2):
        o = b * 8192
        P1B.append(dict(
            ktm=arena[:, o:o + 2048].rearrange("p (s d) -> p s d", s=4),
            vtm=arena[:, o + 2048:o + 6144].rearrange("p (s d) -> p s d", s=4),
            kdecB=arena[:, o + 6144:o + 8192].rearrange("p (c t) -> p c t", c=4),
            lrT=lrT if b == 0 else S(8).bitcast(BF)[:, 0:512]))
    hTs = [hT, ogT]

    for s in range(2):
        dma("sp", xres[:, s, :], ctx_d[s * 128:(s + 1) * 128, :], f"x{s}")
        norm_stats(xres[:, s, :], s, 3 * s)
    norm_T(2, 1, 0, 1)
    for s in range(4):
        xload(xres[:, s, :], NT - 1, s, f"x{s}")
    inproj_kv(2, P1B[0])
    inproj_lr(256, P1B[0])
    gla_D(1, 1, False, P1B[0])
    gla_D(0, 1, False, P1B[0])
    gla_D(0, 0, False, P1B[0])
    gla_D(1, 0, False, P1B[0])
    for c in (0, 1):
        state_update(c, 0, Sf, P1B[0])
    for c in (1, 0):
        state_update(c, 1, Sb, P1B[0])
    copy("act", Sfbf[:, :, :], Sf[:, :, :])
    copy("act", Bnd[:, NT - 1, :], Sb[:, :, :].rearrange("p h v -> p (h v)"))

    def p1_T_units(b):
        return [(lambda j=j: norm_T(4, 0, 0, 0, js=[j], hTv=hTs[b], alt=True)) for j in range(8)]

    def p1_stats_units(i):
        def u(s):
            def f():
                xload(xres[:, s, :], i, s, f"x{s}")
                norm_stats(xres[:, s, :], s, 3 * s)
            return f
        return [u(s) for s in range(4)]

    for u in p1_stats_units(NT - 1):
        u()
    for u in p1_T_units(1):
        u()
    inproj_lr(512, P1B[1], hTs[1])
    inproj_kv(4, P1B[1], hTs[1])
    for u in p1_stats_units(NT - 2):
        u()
    ada_defer.pop(0)()
    for i in range(NT - 1, 0, -1):
        b = i % 2
        Bb = P1B[b]
        Tn = p1_T_units(1 - b)
        KV = kv_units(4, P1B[1 - b], hTs[1 - b]) if i > 1 else []
        ST = p1_stats_units(i - 2) if i >= 2 else []
        for u in Tn:
            u()
        if i > 1:
            inproj_lr(512, P1B[1 - b], hTs[1 - b])
        toks = {}
        toks[3] = gla_Da(3, 1, Bb)
        for n, c in enumerate((3, 2, 1, 0)):
            if KV:
                KV[3 * n]()
            if c > 0:
                toks[c - 1] = gla_Da(c - 1, 1, Bb)
            gla_Db(toks[c], False, Bb)
            if KV:
                KV[3 * n + 1]()
            if c < 3:
                state_update(c + 1, 1, Sb, Bb)
            if KV:
                KV[3 * n + 2]()
            if ST:
                ST[n]()
        state_update(0, 1, Sb, Bb)
        if ada_defer:
            ada_defer.pop(0)()
        copy("act", Bnd[:, i - 1, :], Sb[:, :, :].rearrange("p h v -> p (h v)"))

    while ada_defer:
        ada_defer.pop(0)()
    stt(gmod[:, 2, :], modv[:, 4, :, 0], 1.0, g2_pm, ALU.add, ALU.mult)

    def conv_g1(j):
        w = loadA(9 + 3 * j)
        bank = fm_group(w, lambda kc: hT[:, kc, :])
        act(XA, bank[:, :], AF.Identity, bias=bin_pm[:, 9 + 3 * j:10 + 3 * j])

    def conv_g2(j):
        w = loadA(10 + 3 * j)
        bank = fm_group(w, lambda kc: hT[:, kc, :])
        stt(XA, bank[:, :], bin_pm[:, 10 + 3 * j:11 + 3 * j], XA, ALU.add, ALU.mult)
        act(ACC, XA, AF.Copy, scale=cw_pm[:, j, 1:2])
        u3 = XA.rearrange("p (r w) -> p r w", w=64)
        a3 = ACC.rearrange("p (r w) -> p r w", w=64)
        stt(a3[:, :, 1:64], u3[:, :, 0:63], cw_pm[:, j, 0:1], a3[:, :, 1:64], ALU.mult, ALU.add)
        stt(a3[:, :, 0:63], u3[:, :, 1:64], cw_pm[:, j, 2:3], a3[:, :, 0:63], ALU.mult, ALU.add)

    def conv_g3(j):
        w = loadA(11 + 3 * j)
        bank = fm_group(w, lambda kc: hT[:, kc, :])
        stt(ypre[:, j, :], bank[:, :], bin_pm[:, 11 + 3 * j:12 + 3 * j], ACC, ALU.add, ALU.mult)

    ost_slots = [xst[:, :], scrb[:, 0:2048].bitcast(F32)]

    def final_norm(i, subs=range(4)):
        for s in subs:
            sc = 12 + 3 * s
            act(sq, xres[:, s, :], AF.Square)
            P.add("dve", lambda e, sc=sc: e.tensor_reduce(out=stat[:, sc:sc + 1], in_=sq, axis=AX.X, op=ALU.add),
                  [stat[:, sc:sc + 1]], [sq])
            act(stat[:, sc + 1:sc + 2], stat[:, sc:sc + 1], AF.Ln, bias=epsc[:, 0:1], scale=1.0 / D)
            act(stat[:, sc + 2:sc + 3], stat[:, sc + 1:sc + 2], AF.Exp, scale=-0.5)
            o = state["ost"]
            state["ost"] = 1 - o
            stt(ost_slots[o], xres[:, s, :], stat[:, sc + 2:sc + 3], gfin_bc[:, :], ALU.mult, ALU.mult)
            dma("sp", out_d[i * TT + s * 128:i * TT + (s + 1) * 128, :], ost_slots[o], f"out{o}", track_in=True)

    def scores(c):
        t0 = c * 128
        pf, pb = b_p[c % 2]
        bank = nb()
        for h in range(4):
            mm(bank[:, h * 128:(h + 1) * 128], kT[:, h, t0:t0 + 128], qT[:, h, t0:t0 + 128], True, True)
        tt(v4(pf), v4(bank[:, :]), maskb[:, 0:1, :].broadcast_to([128, 4, 128]), ALU.mult)
        bank = nb()
        for h in range(4):
            mm(bank[:, h * 128:(h + 1) * 128], kbB[:, c, h * 128:(h + 1) * 128],
               qbB[:, c, h * 128:(h + 1) * 128], True, True)
        tt(v4(pb), v4(bank[:, :]), maskb[:, 1:2, :].broadcast_to([128, 4, 128]), ALU.mult)

    def out_T(c):
        t0 = c * 128
        bank = nb()
        bv = bank[:, :].bitcast(BF)
        for j in range(8):
            tr(bv[:, j * 128:(j + 1) * 128], b_on[:, j * 128:(j + 1) * 128])
        copy("dve", ogT[:, :, t0:t0 + 128], bv.rearrange("p (j t) -> p j t", j=8))

    for i in range(NT):
        for h in range(4):
            w = loadA(h)
            bank = fm_group(w, lambda kc: hT[:, kc, :])
            act(qT[:, h, :], bank[:, :], AF.Identity, bias=bq_s[:, h:h + 1], scale=QSCALE)
        for h in range(4):
            w = loadA(4 + h)
            bank = fm_group(w, lambda kc: hT[:, kc, :])
            ts(kT[:, h, :], bank[:, :], bin_pm[:, 4 + h:5 + h], ALU.add)
        inproj_lr(512)
        for s in range(4):
            bank = nb()
            bv = bank[:, :].bitcast(BF)
            for h in range(4):
                tr(bv[:, h * 128:(h + 1) * 128], kT[:, h, s * 128:(s + 1) * 128])
            copy("act", ktm[:, s, :], bv[:, 0:512])
        kvu = kv_units(4, pieces=(1, 2))
        for s in range(4):
            kvu[2 * s]()
            if i > 0:
                final_norm(i - 1, [s])
            xload(xres[:, s, :], i, s, f"x{s}")
            kvu[2 * s + 1]()
        copy("dve", Sb[:, :, :].rearrange("p h v -> p (h v)"), Bnd[:, i, :])
        order = [(3, 1), (2, 1), (1, 1), (0, 1), (0, 0), (1, 0), (2, 0), (3, 0)]
        cgroups = []
        for j in range(4):
            cgroups += [(conv_g1, j), (conv_g2, j), (conv_g3, j)]
        cg = iter(cgroups)

        def next_cg():
            f, j = next(cg)
            f(j)
        tok = gla_Da(*order[0])
        for n, (c, d) in enumerate(order):
            next_cg()
            nxt = gla_Da(*order[n + 1]) if n + 1 < 8 else None
            gla_Db(tok, True)
            tok = nxt
            if 1 <= n <= 4:
                cb = 4 - n
                copy("act", Sbbf[:, cb, :], Sb[:, :, :].rearrange("p h v -> p (h v)"))
                state_update(cb, 1, Sb)
            if n % 2 == 0:
                next_cg()
        state["banks"] = [0, 1, 2, 3]
        scores(0)
        for c in range(4):
            t0 = c * 128
            pf, pb = b_p[c % 2]
            if c < 3:
                scores(c + 1)
            conv_g1(4 + c)
            ob = (ps[4 + 2 * (c % 2)], ps[5 + 2 * (c % 2)])
            for h in range(4):
                reg = ob[h // 2][:, (h % 2) * 256:(h % 2 + 1) * 256]
                vh = vtm[:, c, h * 256:(h + 1) * 256]
                mm(reg, pf[:, h * 128:(h + 1) * 128], vh, True, False)
                mm(reg, qT[:, h, t0:t0 + 128], Sfbf[:, h, :], False, False)
                mm(reg, pb[:, h * 128:(h + 1) * 128], vh, False, False)
                mm(reg, qbB[:, c, h * 128:(h + 1) * 128], Sbbf[:, c, h * 256:(h + 1) * 256], False, True)
            state_update(c, 0, Sf)
            conv_g2(4 + c)
            if c > 0:
                out_T(c - 1)
            copy("act", Sfbf[:, :, :], Sf[:, :, :])
            for hh in range(2):
                act(sq[:, hh * 512:(hh + 1) * 512], ob[hh][:, :], AF.Square)
            P.add("dve", lambda e: e.tensor_reduce(out=stat[:, 36:40], in_=sq.rearrange("p (h v) -> p h v", h=4),
                                                   axis=AX.X, op=ALU.add),
                  [stat[:, 36:40]], [sq])
            act(stat[:, 40:44], stat[:, 36:40], AF.Ln, bias=epsc[:, 0:1], scale=1.0 / 256.0)
            act(stat[:, 44:48], stat[:, 40:44], AF.Exp, scale=-0.5)
            for h in range(4):
                act(b_on[:, h * 256:(h + 1) * 256], ob[h // 2][:, (h % 2) * 256:(h % 2 + 1) * 256],
                    AF.Copy, scale=stat[:, 44 + h:45 + h])
            conv_g3(4 + c)
        out_T(3)
        state["banks"] = list(range(8))
        for j in range(8):
            w = loadA(33 + j)
            bank = fm_group(w, lambda kc: hT[:, kc, :])
            act(S(j % 2), bank[:, :], AF.Silu, bias=bin_pm[:, 33 + j:34 + j])
            stt(ogT[:, j, :], ogT[:, j, :], gg16[:, j:j + 1], S(j % 2), ALU.mult, ALU.mult)
        for j in range(8):
            o4 = 2 + 2 * (j % 2)
            w = loadA(41 + 4 * j)
            bank_a = fm_group(w, lambda kc: ypre[:, kc, :])
            w = loadA(42 + 4 * j)
            bank_b = fm_group(w, lambda kc: ogT[:, kc, :])
            w = loadA(43 + 4 * j)
            bank = fm_group(w, lambda kc: hT[:, kc, :])
            act(S(o4), bank[:, :], AF.Sigmoid, bias=bin_pm[:, 41 + 2 * j:42 + 2 * j])
            w = loadA(44 + 4 * j)
            bank = fm_group(w, lambda kc: hT[:, kc, :])
            act(S(o4 + 1), bank[:, :], AF.Sigmoid, bias=bin_pm[:, 42 + 2 * j:43 + 2 * j])
            tt(S(o4), bank_a[:, :], S(o4), ALU.mult)
            tt(S(o4 + 1), bank_b[:, :], S(o4 + 1), ALU.mult)
            tt(yT[:, j, :], S(o4), S(o4 + 1), ALU.add)
        wo = [loadB(3), loadB(4)]
        for s in range(4):
            for fh in range(2):
                bank = nb()
                for kc in range(8):
                    mm(bank[:, :], yT[:, kc, s * 128:(s + 1) * 128], wo[fh](kc), kc == 0, kc == 7)
                t = S((2 * s + fh) % 4)
                tt(t, bank[:, :], ga_bc[:, 0, fh * 512:(fh + 1) * 512], ALU.mult)
                tt(xres[:, s, fh * 512:(fh + 1) * 512], xres[:, s, fh * 512:(fh + 1) * 512], t, ALU.add)
            norm_stats(xres[:, s, :], s, 12 + 3 * s, on_act=True)
        norm_T_pipelined(2, 3, 0)
        for m in range(32):
            w = loadA(73 + m)
            bank = fm_group(w, lambda kc: hT[:, kc, :])
            act(S(m % 2), bank[:, :], AF.Relu)
            act(aT[:, m, :], S(m % 2), AF.Square)
            if i + 1 < NT and m % 8 == 7:
                s = m // 8
                xload(xst[:, :], i + 1, s, "xst")
                norm_stats(xst[:, :], s, 24 + 3 * s)
        for fh in range(2):
            banks = [nb() for _ in range(4)]
            for g in range(4):
                w = loadB(5 + fh * 4 + g)
                for kcl in range(8):
                    kc = g * 8 + kcl
                    for s in range(4):
                        mm(banks[s][:, :], aT[:, kc, s * 128:(s + 1) * 128], w(kcl), kc == 0, kc == 31)
                if i + 1 < NT:
                    norm_T(4, 0, 0, 0, js=[fh * 4 + g])
            for s in range(4):
                tt(S(2 + s % 2), banks[s][:, :], ga_bc[:, 1, fh * 512:(fh + 1) * 512], ALU.mult)
                tt(xres[:, s, fh * 512:(fh + 1) * 512], xres[:, s, fh * 512:(fh + 1) * 512], S(2 + s % 2), ALU.add)

    final_norm(NT - 1)

    P.emit(final_waits=["out0", "out1"])
    es.close()
    return nc


def _pm(v):
    return np.ascontiguousarray(v.reshape(-1, 128).T)


def _panel(w, c0, ncol, width=None):
    K = w.shape[0]
    width = width or ncol
    out = np.zeros((128, K // 128, width), np.float32)
    out[:, :, :ncol] = w[:, c0:c0 + ncol].reshape(K // 128, 128, ncol).transpose(1, 0, 2)
    return out.reshape(128, -1)


_NC_CACHE = {}


def kernel(x, c, ctx, c_ctx, w_ada, b_ada, g_norm1, w_in, b_in, conv_w, w_conv_out, w_a2_f, b_a_f,
           w_a2_b, b_a_b, g_gla_norm, w_gla_out, w_o, g_norm2, w_up, w_down, g_final):
    f = np.float32
    x = np.asarray(x, f); c = np.asarray(c, f); ctx = np.asarray(ctx, f); c_ctx = np.asarray(c_ctx, f)
    w_ada = np.asarray(w_ada, f)[0]; b_ada = np.asarray(b_ada, f)[0]
    g1 = np.asarray(g_norm1, f)[0]; w_in = np.asarray(w_in, f)[0]; b_in = np.asarray(b_in, f)[0]
    conv_w = np.asarray(conv_w, f)[0]; w_co = np.asarray(w_conv_out, f)[0]
    waf = np.asarray(w_a2_f, f)[0]; baf = np.asarray(b_a_f, f)[0]
    wab = np.asarray(w_a2_b, f)[0]; bab = np.asarray(b_a_b, f)[0]
    gg = np.asarray(g_gla_norm, f)[0]; w_go = np.asarray(w_gla_out, f)[0]; w_o = np.asarray(w_o, f)[0]
    g2 = np.asarray(g_norm2, f)[0]; w_up = np.asarray(w_up, f)[0]; w_down = np.asarray(w_down, f)[0]
    g_final = np.asarray(g_final, f)

    XA, BA, CA, Q, K_, V, R, LRF, LRB, GA, GB = 0, 1024, 2048, 3072, 3584, 4096, 5120, 6144, 6160, 6176, 7200

    fm = []
    fm_bias = []
    for h in range(4):
        fm.append(_panel(w_in, Q + h * 128, 128)); fm_bias.append(b_in[Q + h * 128:Q + (h + 1) * 128])
    for h in range(4):
        fm.append(_panel(w_in, K_ + h * 128, 128)); fm_bias.append(b_in[K_ + h * 128:K_ + (h + 1) * 128])
    lrw = np.zeros((1024, 128), f)
    lrw[:, 0:16] = w_in[:, LRF:LRF + 16]
    lrw[:, 32:48] = w_in[:, LRB:LRB + 16]
    lrb = np.zeros(128, f)
    lrb[0:16] = b_in[LRF:LRF + 16]; lrb[16] = 1.0
    lrb[32:48] = b_in[LRB:LRB + 16]; lrb[48] = 1.0
    fm.append(_panel(lrw, 0, 128)); fm_bias.append(lrb)
    for j in range(8):
        for base in (XA, CA, BA):
            fm.append(_panel(w_in, base + j * 128, 128)); fm_bias.append(b_in[base + j * 128:base + (j + 1) * 128])
    for j in range(8):
        fm.append(_panel(w_in, R + j * 128, 128)); fm_bias.append(b_in[R + j * 128:R + (j + 1) * 128])
    gate_bias = []
    for j in range(8):
        fm.append(_panel(w_co, j * 128, 128))
        fm.append(_panel(w_go, j * 128, 128))
        fm.append(_panel(w_in, GA + j * 128, 128)); gate_bias.append(b_in[GA + j * 128:GA + (j + 1) * 128])
        fm.append(_panel(w_in, GB + j * 128, 128)); gate_bias.append(b_in[GB + j * 128:GB + (j + 1) * 128])
    for m in range(32):
        fm.append(_panel(w_up, m * 128, 128))
    wfm = np.stack(fm)
    assert wfm.shape == (NFM, 128, 1024)
    bin_pm = np.stack(fm_bias + gate_bias, axis=1)
    assert bin_pm.shape == (128, 57)

    tm = [_panel(w_in, K_, 512), _panel(w_in, V, 512), _panel(w_in, V + 512, 512),
          _panel(w_o, 0, 512), _panel(w_o, 512, 512)]
    for fh in range(2):
        for g in range(4):
            tm.append(_panel(w_down[g * 1024:(g + 1) * 1024], fh * 512, 512))
    wtm = np.stack(tm)
    assert wtm.shape == (NTM, 128, 4096)
    binkv = np.concatenate([b_in[K_:K_ + 512], b_in[V:V + 1024]])[None, :]

    wada = np.stack([_panel(w_ada, j * 512, 512) for j in range(12)])
    smallpm = np.concatenate([
        _pm(b_ada), bin_pm, _pm(g1), _pm(g2), _pm(gg),
        np.stack([_pm(conv_w[i]) for i in range(3)], axis=2).reshape(128, 24),
    ], axis=1).astype(f)
    assert smallpm.shape == (128, 153)

    wa2 = np.zeros((2, 128, 512), f)
    wa2[0, 0:16] = waf; wa2[0, 16] = baf
    wa2[1, 32:48] = wab; wa2[1, 48] = bab

    gfin = np.ascontiguousarray(np.broadcast_to(g_final[None, :], (128, 1024))).astype(f)

    s_i = np.arange(128)[:, None]
    t_i = np.arange(128)[None, :]
    ng = -1.0 / 16.0
    cst = np.stack([
        (s_i <= t_i) * ng, (s_i >= t_i) * ng, (s_i > t_i) * ng, (s_i < t_i) * ng,
        (s_i <= t_i) * 1.0, (s_i >= t_i) * 1.0, (s_i == t_i) * 1.0, np.ones((128, 128)),
    ], axis=1).astype(f).reshape(128, 8 * 128)

    shared = dict(wada=wada, badarow=b_ada[None, :].copy(), smallpm=smallpm, binkv=binkv.astype(f),
                  wfm=wfm, wtm=wtm, wa2=wa2, gfin=gfin, cst=cst)
    in_maps = []
    for b in range(8):
        cc = np.stack([_pm(c[b]), _pm(c_ctx)], axis=2).reshape(128, 16).astype(f)
        m = dict(shared)
        m.update(x=np.ascontiguousarray(x[b]), ctx=np.ascontiguousarray(ctx[b]), cc=cc)
        in_maps.append(m)

    if "nc" not in _NC_CACHE:
        _NC_CACHE["nc"] = build_nc()
    res = run_bass_kernel_spmd(_NC_CACHE["nc"], in_maps, core_ids=list(range(8)))
    return np.stack([np.asarray(r["out"], dtype=np.float32) for r in res.results], axis=0)
```

```python
import numpy as np
import concourse.bass as bass
import concourse.mybir as mybir
from concourse.bass_utils import run_bass_kernel_spmd
from contextlib import ExitStack

F32 = mybir.dt.float32
BF = mybir.dt.bfloat16
AF = mybir.ActivationFunctionType
ALU = mybir.AluOpType
AX = mybir.AxisListType

D = 1024
SEQ = 4096
CTX = 256
TT = 512
NT = SEQ // TT
NFM = 105
NTM = 13
EPS = 1e-6
QSCALE = 128.0 ** -0.5

CELL = 256
SEM_CAP = 20000


def ap_cells(ap):
    name = ap.tensor.name
    if name.startswith("ps"):
        return {(name, 0)}
    es = mybir.dt.size(ap.dtype)
    pat = ap.ap
    pstep = pat[0][0]
    off = int(ap.offset)
    base = (off % pstep) * es if pstep else off * es
    dims = [(s, c) for (s, c) in pat[1:] if c > 1]
    if not dims:
        dims = [(1, 1)]
    *outer, (ls, lc) = dims
    if ls == 1:
        run = lc * es
    elif ls == 0:
        run = es
    else:
        run = ((lc - 1) * abs(ls) + 1) * es
    n = 1
    for (s, c) in outer:
        n *= c
    if n > 256:
        lo = base
        hi = base + run
        for (s, c) in outer:
            hi += (c - 1) * abs(s) * es
        return {(name, k) for k in range(lo // CELL, (hi - 1) // CELL + 1)}
    starts = [base]
    for (s, c) in outer:
        starts = [b + i * s * es for b in starts for i in range(c)]
    cells = set()
    for b in starts:
        for k in range(b // CELL, (b + run - 1) // CELL + 1):
            cells.add((name, k))
    return cells


class Op:
    __slots__ = ("idx", "eng", "fn", "wcells", "rcells", "dma", "dkey", "waits",
                 "sig", "sigval", "dcount")

    def __init__(self):
        self.waits = []
        self.sig = False
        self.sigval = None
        self.dcount = None


class Prog:
    def __init__(self, nc):
        self.nc = nc
        self.ops = []

    def add(self, eng, fn, outs, ins, dma=False, dkey=None):
        op = Op()
        op.idx = len(self.ops)
        op.eng = eng
        op.fn = fn
        op.dma = dma
        op.dkey = dkey
        w = set()
        r = set()
        for a in outs:
            w |= ap_cells(a)
        for a in ins:
            r |= ap_cells(a)
        op.wcells = w
        op.rcells = r
        self.ops.append(op)
        return op

    def resolve(self):
        last_w = {}
        readers = {}
        known = {}
        dcounts = {}
        for op in self.ops:
            deps = {}
            for c in op.rcells:
                j = last_w.get(c)
                if j is not None:
                    deps[j] = "raw"
                if c[0].startswith("ps"):
                    rd = readers.get(c)
                    if rd:
                        for e2, j2 in rd.items():
                            if e2 != op.eng and j2 not in deps:
                                deps[j2] = "rar"
            for c in op.wcells:
                j = last_w.get(c)
                if j is not None and j not in deps:
                    deps[j] = "waw"
                rd = readers.get(c)
                if rd:
                    for j in rd.values():
                        if j not in deps:
                            deps[j] = "war"
            for c in op.rcells:
                if c in op.wcells:
                    continue
                d = readers.get(c)
                if d is None:
                    d = readers[c] = {}
                d[("dma", op.idx) if op.dma else op.eng] = op.idx
            for c in op.wcells:
                last_w[c] = op.idx
                readers[c] = {}
            if op.dma:
                dcounts[op.dkey] = dcounts.get(op.dkey, 0) + 1
                op.dcount = dcounts[op.dkey]
            kn = known.setdefault(op.eng, {})
            for j, kind in deps.items():
                p = self.ops[j]
                if p.dma:
                    key = ("dma", p.dkey)
                    if kn.get(key, 0) >= p.dcount:
                        continue
                    kn[key] = p.dcount
                    op.waits.append(("dma", p.dkey, p.dcount))
                else:
                    if (not op.dma) and p.eng == op.eng and op.eng == "pe":
                        continue
                    key = ("eng", p.eng)
                    if kn.get(key, -1) >= j:
                        continue
                    kn[key] = j
                    p.sig = True
                    op.waits.append(("eng", j))
        self.dtotals = dcounts
        cnt = {}
        for op in self.ops:
            if op.sig and not op.dma:
                k = cnt.get(op.eng, 0)
                op.sigval = (k // SEM_CAP, k % SEM_CAP + 1)
                cnt[op.eng] = k + 1
        self.sigcounts = cnt

    def emit(self, final_waits=()):
        nc = self.nc
        self.resolve()
        with ExitStack() as es:
            esem = {}
            for e, k in self.sigcounts.items():
                esem[e] = [es.enter_context(nc.semaphore(f"sa_{e}{i}"))
                           for i in range((k - 1) // SEM_CAP + 1)]
            dsem = {key: es.enter_context(nc.semaphore(f"d_{key}")) for key in self.dtotals}
            block = es.enter_context(nc.Block())
            queues = {"pe": block.tensor, "act": block.scalar, "dve": block.vector,
                      "pool": block.gpsimd, "sp": block.sync}
            by_eng = {q: [] for q in queues}
            for op in self.ops:
                by_eng[op.eng].append(op)

            def make(qname):
                ops = by_eng[qname]

                def body(eng):
                    for op in ops:
                        for w in op.waits:
                            if w[0] == "eng":
                                p = self.ops[w[1]]
                                si, v = p.sigval
                                eng.wait_ge(esem[p.eng][si], v)
                            else:
                                eng.wait_ge(dsem[w[1]], 16 * w[2])
                        ins = op.fn(eng)
                        if op.dma:
                            ins.then_inc(dsem[op.dkey], 16)
                        elif op.sig:
                            si, v = op.sigval
                            ins.then_inc(esem[op.eng][si], 1)
                    if qname == "sp":
                        for key in final_waits:
                            eng.wait_ge(dsem[key], 16 * self.dtotals[key])
                return body

            for qname, deco in queues.items():
                if by_eng[qname] or qname == "sp":
                    deco(make(qname))


def build_nc():
    nc = bass.Bass("TRN2", target_bir_lowering=False)

    def din(name, shape):
        return nc.dram_tensor(name, list(shape), F32, kind="ExternalInput").ap()

    x_d = din("x", [SEQ, D])
    ctx_d = din("ctx", [CTX, D])
    cc_d = din("cc", [128, 16])
    wada_d = din("wada", [12, 128, 8 * 512])
    badarow_d = din("badarow", [1, 6144])
    smallpm_d = din("smallpm", [128, 48 + 57 + 8 + 8 + 8 + 24])
    binkv_d = din("binkv", [1, 1536])
    wfm_d = din("wfm", [NFM, 128, 1024])
    wtm_d = din("wtm", [NTM, 128, 4096])
    wa2_d = din("wa2", [2, 128, 512])
    gfin_d = din("gfin", [128, 1024])
    cst_d = din("cst", [128, 8 * 128])
    out_d = nc.dram_tensor("out", [SEQ, D], F32, kind="ExternalOutput").ap()

    es = ExitStack()

    def sb(name, shape, dt):
        return es.enter_context(nc.sbuf_tensor("sb_" + name, list(shape), dt))

    xres = sb("xres", [128, 4, 1024], F32)
    xst = sb("xst", [128, 1024], F32)
    hT = sb("hT", [128, 8, 512], BF)
    ogT = sb("ogT", [128, 8, 512], BF)
    arena = sb("arena", [128, 16384], BF)
    xn = sb("xn", [128, 4, 1024], BF)
    scr = sb("scr", [128, 6144], F32)
    scrb = sb("scrb", [128, 3072], BF)
    Sf = sb("Sf", [128, 4, 256], F32)
    Sb = sb("Sb", [128, 4, 256], F32)
    Sfbf = sb("Sfbf", [128, 4, 256], BF)
    Sbbf = sb("Sbbf", [128, 4, 1024], BF)
    Bnd = sb("Bnd", [128, 8, 1024], BF)
    ringA = sb("ringA", [128, 8, 1024], BF)
    ringB = sb("ringB", [128, 4, 2048], BF)
    cst = sb("cst", [128, 8, 128], F32)
    identb = sb("identb", [128, 128], BF)
    maskb = sb("maskb", [128, 2, 128], BF)
    binkv_bc = sb("binkv_bc", [128, 1536], F32)
    ga_bc = sb("ga_bc", [128, 2, 1024], F32)
    gfin_bc = sb("gfin_bc", [128, 1024], F32)
    wa2 = sb("wa2", [128, 2, 512], BF)
    lrT = sb("lrT", [128, 512], BF)
    tri_bf = sb("tri_bf", [128, 4, 128], BF)
    smallpm = sb("smallpm", [128, 153], F32)
    cc = sb("cc", [128, 16], F32)
    sil = sb("sil", [128, 16], F32)
    modpm = sb("modpm", [128, 6 * 8 * 2], F32)
    gmod = sb("gmod", [128, 3, 8], F32)
    gg16 = sb("gg16", [128, 8], F32)
    stat = sb("stat", [128, 48], F32)
    decs = sb("decs", [128, 32], F32)
    epsc = sb("epsc", [128, 1], F32)
    bq_s = sb("bq_s", [128, 4], F32)

    ps = [es.enter_context(nc.psum_tensor(f"ps{i}", [128, 512], F32)) for i in range(8)]

    qT = arena[:, 0:2048].rearrange("p (h t) -> p h t", h=4)
    kT = arena[:, 2048:4096].rearrange("p (h t) -> p h t", h=4)
    ktm = arena[:, 4096:6144].rearrange("p (s d) -> p s d", s=4)
    vtm = arena[:, 6144:10240].rearrange("p (s d) -> p s d", s=4)
    qbB = arena[:, 10240:12288].rearrange("p (c t) -> p c t", c=4)
    kbB = arena[:, 12288:14336].rearrange("p (c t) -> p c t", c=4)
    kdecB = arena[:, 14336:16384].rearrange("p (c t) -> p c t", c=4)
    yT = arena[:, 4096:8192].rearrange("p (k t) -> p k t", k=8)
    aT = arena[:, :].rearrange("p (k t) -> p k t", k=32)
    arena_f = arena[:, :].bitcast(F32)
    wstage = arena_f.rearrange("p (s k c) -> p s k c", s=2, k=8)
    ypre = xn[:, :, :].rearrange("p s f -> p (s f)").rearrange("p (k t) -> p k t", k=8)
    xn_f = xn[:, :, :].rearrange("p s f -> p (s f)").bitcast(F32).rearrange("p (o f) -> p o f", o=2)

    def S(i):
        return scr[:, i * 512:(i + 1) * 512]
    XA = S(8)
    ACC = S(9)
    sq = scr[:, 5120:6144]
    b_p = [[scrb[:, 0:512], scrb[:, 512:1024]], [scrb[:, 1024:1536], scrb[:, 1536:2048]]]
    b_on = scrb[:, 2048:3072]
    sil_rep = scrb[:, 0:1024].rearrange("p (k m) -> p k m", k=8)

    bada_pm = smallpm[:, 0:48]
    bin_pm = smallpm[:, 48:105]
    g1_pm = smallpm[:, 105:113]
    g2_pm = smallpm[:, 113:121]
    gg_pm = smallpm[:, 121:129]
    cw_pm = smallpm[:, 129:153].rearrange("p (j i) -> p j i", i=3)
    modv = modpm[:, :].rearrange("p (g j l) -> p g j l", g=6, j=8)
    sil3 = sil[:, :].rearrange("p (k l) -> p k l", l=2)
    TRI_U, TRI_L, TRI_SL, TRI_SU, C_U01, C_L01, C_ID, C_ONE = range(8)

    P = Prog(nc)
    state = {"bank": 0, "ra": 0, "rb": 0, "ost": 0, "dset": 0}

    state["banks"] = list(range(8))

    def nb():
        bl = state["banks"]
        b = bl[state["bank"] % len(bl)]
        state["bank"] += 1
        return ps[b]

    def dma(q, out, in_, key, track_in=False):
        P.add(q, lambda e: e.dma_start(out=out, in_=in_), [out] if out.tensor.name != "out" else [],
              [in_] if track_in else [], dma=True, dkey=key)

    def mm(out, lhsT, rhs, start, stop):
        P.add("pe", lambda e: e.matmul(out, lhsT, rhs, start=start, stop=stop), [out], [lhsT, rhs])

    def tr(out, in_):
        P.add("pe", lambda e: e.transpose(out, in_, identb[:, :]), [out], [in_, identb[:, :]])

    def act(out, in_, func, bias=None, scale=None):
        ins = [in_]
        kw = {}
        if bias is not None:
            kw["bias"] = bias
            if not isinstance(bias, float):
                ins.append(bias)
        if scale is not None:
            kw["scale"] = scale
            if not isinstance(scale, float):
                ins.append(scale)
        P.add("act", lambda e: e.activation(out=out, in_=in_, func=func, **kw), [out], ins)

    def tt(out, in0, in1, op, eng="dve"):
        P.add(eng, lambda e: e.tensor_tensor(out=out, in0=in0, in1=in1, op=op), [out], [in0, in1])

    def ts(out, in0, s1, op0, s2=None, op1=None, eng="dve"):
        ins = [in0] + [s for s in (s1, s2) if s is not None and not isinstance(s, float)]
        if op1 is None:
            P.add(eng, lambda e: e.tensor_scalar(out=out, in0=in0, scalar1=s1, scalar2=None, op0=op0),
                  [out], ins)
        else:
            P.add(eng, lambda e: e.tensor_scalar(out=out, in0=in0, scalar1=s1, scalar2=s2, op0=op0, op1=op1),
                  [out], ins)

    def stt(out, in0, scalar, in1, op0, op1):
        ins = [in0, in1] + ([] if isinstance(scalar, float) else [scalar])
        P.add("dve", lambda e: e.scalar_tensor_tensor(out=out, in0=in0, scalar=scalar, in1=in1, op0=op0, op1=op1),
              [out], ins)

    def copy(eng, out, in_):
        if eng == "act":
            P.add("act", lambda e: e.copy(out=out, in_=in_), [out], [in_])
        else:
            P.add(eng, lambda e: e.tensor_copy(out=out, in_=in_), [out], [in_])

    def memset(out, val, eng="dve"):
        P.add(eng, lambda e: e.memset(out, val), [out], [])

    def loadA(idx):
        s = state["ra"]
        state["ra"] = (s + 1) % 8
        dma("pool", ringA[:, s, :], wfm_d[idx], f"ra{s}")
        return ringA[:, s, :].rearrange("p (k c) -> p k c", k=8)

    def loadB(idx):
        halves = []
        for hf in range(2):
            s = state["rb"]
            state["rb"] = (s + 1) % 4
            dma("pool", ringB[:, s, :], wtm_d[idx][:, hf * 2048:(hf + 1) * 2048], f"rb{s}")
            halves.append(ringB[:, s, :].rearrange("p (k c) -> p k c", k=4))
        return lambda kc: halves[kc // 4][:, kc % 4, :]

    def fm_group(panel, rhs_fn, ntok=512):
        bank = nb()
        for kc in range(8):
            mm(bank[:, 0:ntok], panel[:, kc, :], rhs_fn(kc), kc == 0, kc == 7)
        return bank

    dma("sp", cst[:, :, :], cst_d.rearrange("p (a b) -> p a b", a=8), "setup0")
    dma("sp", smallpm[:, :], smallpm_d, "setup1")
    dma("sp", cc[:, :], cc_d, "setup2")
    dma("sp", ga_bc[:, 0, :], badarow_d[0:1, 2048:3072].broadcast_to([128, 1024]), "setup3")
    dma("sp", ga_bc[:, 1, :], badarow_d[0:1, 5120:6144].broadcast_to([128, 1024]), "setup3b")
    dma("sp", binkv_bc[:, :], binkv_d[0:1, :].broadcast_to([128, 1536]), "setup4")
    dma("pool", wa2[:, :, :], wa2_d.rearrange("a p c -> p a c"), "setup5")
    dma("sp", gfin_bc[:, :], gfin_d, "setup6")

    copy("dve", identb[:, :], cst[:, C_ID, :])
    copy("dve", tri_bf[:, :, :], cst[:, 0:4, :])
    copy("dve", maskb[:, 0, :], cst[:, C_U01, :])
    copy("dve", maskb[:, 1, :], cst[:, C_L01, :])
    ones = cst[:, C_ONE, :]
    act(sil[:, :], cc[:, :], AF.Silu)
    for kc in range(8):
        ts(sil_rep[:, kc, :], ones, sil3[:, kc, 0:1], ALU.mult)
    ts(gg16[:, :], gg_pm, 1.0, ALU.mult)
    memset(epsc[:, :], EPS)
    ts(bq_s[:, :], bin_pm[:, 0:4], QSCALE, ALU.mult)
    memset(Sf[:, :, :], 0.0)
    memset(Sb[:, :, :], 0.0)

    sil_bf = sb("sil_bf", [128, 16], BF)
    copy("dve", sil_bf[:, :], sil[:, :])
    silb3 = sil_bf[:, :].rearrange("p (k l) -> p k l", l=2)
    ada_stage = [arena[:, 0:4096].rearrange("p (k c) -> p k c", k=8),
                 arena[:, 4096:8192].rearrange("p (k c) -> p k c", k=8),
                 Sbbf[:, :, :].rearrange("p a b -> p (a b)").rearrange("p (k c) -> p k c", k=8)]

    def ada_piece(j, slot):
        g = j // 2
        st = ada_stage[slot]
        dma("pool", st, wada_d[j].rearrange("p (k c) -> p k c", k=8), f"wst{slot}")
        if g in (2, 5):
            bank = nb()
            for kc in range(8):
                mm(bank[:, :], sil_rep[:, kc, :], st[:, kc, :], kc == 0, kc == 7)
            gi = 0 if g == 2 else 1
            dst = ga_bc[:, gi, (j % 2) * 512:(j % 2 + 1) * 512]
            tt(dst, bank[:, :], dst, ALU.add)
        else:
            bank = nb()
            for q in range(4):
                for kc in range(8):
                    mm(bank[:, q * 2:q * 2 + 2], st[:, kc, q * 128:(q + 1) * 128], silb3[:, kc, :],
                       kc == 0, kc == 7)
            for q in range(4):
                ch = (j % 2) * 4 + q
                ts(modv[:, g, ch, :], bank[:, q * 2:q * 2 + 2], bada_pm[:, g * 8 + ch:g * 8 + ch + 1], ALU.add)

    for j in range(4):
        ada_piece(j, j % 2)
    stt(gmod[:, 0, :], modv[:, 1, :, 0], 1.0, g1_pm, ALU.add, ALU.mult)
    stt(gmod[:, 1, :], modv[:, 1, :, 1], 1.0, g1_pm, ALU.add, ALU.mult)
    ada_defer = [(lambda j=j: ada_piece(j, 2)) for j in range(4, 12)]

    def norm_stats(src, s, sc, on_act=False):
        act(sq, src, AF.Square)
        P.add("dve", lambda e: e.tensor_reduce(out=stat[:, sc:sc + 1], in_=sq, axis=AX.X, op=ALU.add),
              [stat[:, sc:sc + 1]], [sq])
        act(stat[:, sc + 1:sc + 2], stat[:, sc:sc + 1], AF.Ln, bias=epsc[:, 0:1], scale=1.0 / D)
        act(stat[:, sc + 2:sc + 3], stat[:, sc + 1:sc + 2], AF.Exp, scale=-0.5)
        if on_act:
            act(xn[:, s, :], src, AF.Copy, scale=stat[:, sc + 2:sc + 3])
        else:
            ts(xn[:, s, :], src, stat[:, sc + 2:sc + 3], ALU.mult)

    def norm_T(nsub, gm_idx, sh_g, sh_l, js=range(8), hTv=None, alt=False):
        hTv = hT if hTv is None else hTv
        nt = nsub * 128
        for j in js:
            bank = nb()
            bv = bank[:, :].bitcast(BF)
            for s in range(nsub):
                tr(bv[:, s * 128:(s + 1) * 128], xn[:, s, j * 128:(j + 1) * 128])
            if alt and j % 2 == 1:
                ts(hTv[:, j, 0:nt], bv[:, 0:nt], gmod[:, gm_idx, j:j + 1], ALU.mult,
                   modv[:, sh_g, j, sh_l:sh_l + 1], ALU.add)
            else:
                act(hTv[:, j, 0:nt], bv[:, 0:nt], AF.Identity, bias=modv[:, sh_g, j, sh_l:sh_l + 1],
                    scale=gmod[:, gm_idx, j:j + 1])

    MAINB = dict(ktm=ktm, vtm=vtm, kdecB=kdecB, lrT=lrT)

    def kv_units(nsub, B=None, hTv=None, pieces=(0, 1, 2)):
        B = MAINB if B is None else B
        hTv = hT if hTv is None else hTv
        units = []
        holder = {}

        def unit(piece, s):
            def f():
                if s == 0:
                    holder[piece] = loadB(piece)
                w = holder[piece]
                bank = nb()
                for kc in range(8):
                    mm(bank[:, :], hTv[:, kc, s * 128:(s + 1) * 128], w(kc), kc == 0, kc == 7)
                if piece == 0:
                    tt(B["ktm"][:, s, :], bank[:, :], binkv_bc[:, 0:512], ALU.add)
                else:
                    tt(B["vtm"][:, s, (piece - 1) * 512:piece * 512], bank[:, :],
                       binkv_bc[:, piece * 512:(piece + 1) * 512], ALU.add)
            return f
        for piece in pieces:
            for s in range(nsub):
                units.append(unit(piece, s))
        return units

    def inproj_kv(nsub, B=None, hTv=None, pieces=(0, 1, 2)):
        for u in kv_units(nsub, B, hTv, pieces):
            u()

    def norm_T_pipelined(gm_idx, sh_g, sh_l):
        banks = [nb() for _ in range(4)]
        bvs = [b[:, :].bitcast(BF) for b in banks]
        for s in range(4):
            for j in range(8):
                tr(bvs[j // 2][:, (j % 2) * 512 + s * 128:(j % 2) * 512 + (s + 1) * 128],
                   xn[:, s, j * 128:(j + 1) * 128])
        for j in range(8):
            src = bvs[j // 2][:, (j % 2) * 512:(j % 2 + 1) * 512]
            if (j // 2) % 2 == 0:
                act(hT[:, j, :], src, AF.Identity, bias=modv[:, sh_g, j, sh_l:sh_l + 1],
                    scale=gmod[:, gm_idx, j:j + 1])
            else:
                ts(hT[:, j, :], src, gmod[:, gm_idx, j:j + 1], ALU.mult, modv[:, sh_g, j, sh_l:sh_l + 1], ALU.add)

    def inproj_lr(nt, B=None, hTv=None):
        B = MAINB if B is None else B
        hTv = hT if hTv is None else hTv
        w = loadA(8)
        bank = fm_group(w, lambda kc: hTv[:, kc, 0:nt], nt)
        act(B["lrT"][:, 0:nt], bank[:, 0:nt], AF.Identity, bias=bin_pm[:, 8:9])

    def v4(ap512):
        return ap512.rearrange("p (h t) -> p h t", h=4)

    def gla_Da(c, d, B=None):
        B = MAINB if B is None else B
        ds_ = state["dset"]
        state["dset"] = 1 - ds_
        SP = S(4 * ds_)
        SPh = S(4 * ds_ + 1).bitcast(BF)[:, 512:1024]
        SPl = S(4 * ds_ + 2).bitcast(BF)[:, 512:1024]
        t0 = c * 128
        bank = nb()
        mm(bank[:, :], B["lrT"][:, t0:t0 + 128], wa2[:, d, :], True, True)
        act(SP, bank[:, :], AF.Exp, scale=-1.0)
        act(SPh, SP, AF.Ln, bias=1.0)
        return (c, d, ds_)

    def gla_Db(tok, with_qk, B=None):
        B = MAINB if B is None else B
        c, d, ds_ = tok
        ktm_, kdecB_ = B["ktm"], B["kdecB"]
        SP = S(4 * ds_)
        EREM = S(4 * ds_ + 1).bitcast(BF)[:, 0:512]
        EB = S(4 * ds_ + 2).bitcast(BF)[:, 0:512]
        ENB = S(4 * ds_ + 3).bitcast(BF)[:, 0:512]
        SPh = S(4 * ds_ + 1).bitcast(BF)[:, 512:1024]
        SPl = S(4 * ds_ + 2).bitcast(BF)[:, 512:1024]
        t0 = c * 128
        triS = tri_bf[:, TRI_SL if d == 0 else TRI_SU, :]
        triI = tri_bf[:, TRI_U if d == 0 else TRI_L, :]
        bank2 = nb()
        mm(bank2[:, :], triS, SPh, True, True)
        bank3 = nb()
        for h in range(4):
            mm(bank3[:, h * 128:(h + 1) * 128], SPh[:, h * 128:(h + 1) * 128], triI, True, True)
        act(EREM, bank2[:, :], AF.Exp)
        col = 127 if d == 0 else 0
        act(decs[:, (c * 2 + d) * 4:(c * 2 + d) * 4 + 4], v4(bank3[:, :])[:, :, col], AF.Exp)
        if d == 1:
            tt(kdecB_[:, c, :], ktm_[:, c, :], EREM, ALU.mult)
        else:
            tt(ktm_[:, c, :], ktm_[:, c, :], EREM, ALU.mult)
        if with_qk:
            act(EB, bank3[:, :], AF.Exp)
            act(ENB, bank3[:, :], AF.Exp, scale=-1.0)
            if d == 1:
                tt(v4(qbB[:, c, :]), qT[:, :, t0:t0 + 128], v4(EB), ALU.mult)
                tt(v4(kbB[:, c, :]), kT[:, :, t0:t0 + 128], v4(ENB), ALU.mult)
            else:
                tt(qT[:, :, t0:t0 + 128], qT[:, :, t0:t0 + 128], v4(EB), ALU.mult)
                tt(kT[:, :, t0:t0 + 128], kT[:, :, t0:t0 + 128], v4(ENB), ALU.mult)

    def gla_D(c, d, with_qk, B=None):
        gla_Db(gla_Da(c, d, B), with_qk, B)

    def state_update(c, d, Sst, B=None):
        B = MAINB if B is None else B
        kd = B["kdecB"][:, c, :] if d == 1 else B["ktm"][:, c, :]
        banks = (nb(), nb())
        for h in range(4):
            mm(banks[h // 2][:, (h % 2) * 256:(h % 2 + 1) * 256], kd[:, h * 128:(h + 1) * 128],
               B["vtm"][:, c, h * 256:(h + 1) * 256], True, True)
        for h in range(4):
            i0 = (c * 2 + d) * 4 + h
            stt(Sst[:, h, :], Sst[:, h, :], decs[:, i0:i0 + 1],
                banks[h // 2][:, (h % 2) * 256:(h % 2 + 1) * 256], ALU.mult, ALU.add)

    def xload(dst, i, s, key):
        dma("sp", dst, x_d[i * TT + s * 128:i * TT + (s + 1) * 128, :], key)

    P1B = []
    for b in range(2):
        o = b * 8192
        P1B.append(dict(
            ktm=arena[:, o:o + 2048].rearrange("p (s d) -> p s d", s=4),
            vtm=arena[:, o + 2048:o + 6144].rearrange("p (s d) -> p s d", s=4),
            kdecB=arena[:, o + 6144:o + 8192].rearrange("p (c t) -> p c t", c=4),
            lrT=lrT if b == 0 else S(8).bitcast(BF)[:, 0:512]))
    hTs = [hT, ogT]

    def p1_T_units(b):
        return [(lambda j=j: norm_T(4, 0, 0, 0, js=[j], hTv=hTs[b], alt=True)) for j in range(8)]

    def p1_stats_units(i):
        def u(s):
            def f():
                xload(xres[:, s, :], i, s, f"x{s}")
                norm_stats(xres[:, s, :], s, 3 * s)
            return f
        return [u(s) for s in range(4)]

    for s in range(2):
        dma("sp", xres[:, s, :], ctx_d[s * 128:(s + 1) * 128, :], f"x{s}")
        norm_stats(xres[:, s, :], s, 3 * s)
    norm_T(2, 1, 0, 1)
    for u in p1_stats_units(NT - 1):
        u()
    inproj_kv(2, P1B[0])
    inproj_lr(256, P1B[0])
    for u in p1_T_units(1):
        u()
    KV7 = kv_units(4, P1B[1], hTs[1])
    for n, (c, d) in enumerate([(1, 1), (0, 1), (0, 0), (1, 0)]):
        gla_D(c, d, False, P1B[0])
        for f in KV7[3 * n:3 * n + 3]:
            f()
    inproj_lr(512, P1B[1], hTs[1])
    for c in (0, 1):
        state_update(c, 0, Sf, P1B[0])
    for c in (1, 0):
        state_update(c, 1, Sb, P1B[0])
    copy("act", Sfbf[:, :, :], Sf[:, :, :])
    copy("act", Bnd[:, NT - 1, :], Sb[:, :, :].rearrange("p h v -> p (h v)"))
    for u in p1_stats_units(NT - 2):
        u()
    ada_defer.pop(0)()
    for i in range(NT - 1, 0, -1):
        b = i % 2
        Bb = P1B[b]
        Tn = p1_T_units(1 - b)
        KV = kv_units(4, P1B[1 - b], hTs[1 - b]) if i > 1 else []
        ST = p1_stats_units(i - 2) if i >= 2 else []
        for u in Tn:
            u()
        if i > 1:
            inproj_lr(512, P1B[1 - b], hTs[1 - b])
        toks = {}
        toks[3] = gla_Da(3, 1, Bb)
        for n, c in enumerate((3, 2, 1, 0)):
            if KV:
                KV[3 * n]()
            if c > 0:
                toks[c - 1] = gla_Da(c - 1, 1, Bb)
            gla_Db(toks[c], False, Bb)
            if KV:
                KV[3 * n + 1]()
            if c < 3:
                state_update(c + 1, 1, Sb, Bb)
            if KV:
                KV[3 * n + 2]()
            if ST:
                ST[n]()
        state_update(0, 1, Sb, Bb)
        if ada_defer:
            ada_defer.pop(0)()
        copy("act", Bnd[:, i - 1, :], Sb[:, :, :].rearrange("p h v -> p (h v)"))

    while ada_defer:
        ada_defer.pop(0)()
    stt(gmod[:, 2, :], modv[:, 4, :, 0], 1.0, g2_pm, ALU.add, ALU.mult)

    def conv_g1(j):
        w = loadA(9 + 3 * j)
        bank = fm_group(w, lambda kc: hT[:, kc, :])
        act(XA, bank[:, :], AF.Identity, bias=bin_pm[:, 9 + 3 * j:10 + 3 * j])

    def conv_g2(j):
        w = loadA(10 + 3 * j)
        bank = fm_group(w, lambda kc: hT[:, kc, :])
        stt(XA, bank[:, :], bin_pm[:, 10 + 3 * j:11 + 3 * j], XA, ALU.add, ALU.mult)
        act(ACC, XA, AF.Copy, scale=cw_pm[:, j, 1:2])
        u3 = XA.rearrange("p (r w) -> p r w", w=64)
        a3 = ACC.rearrange("p (r w) -> p r w", w=64)
        stt(a3[:, :, 1:64], u3[:, :, 0:63], cw_pm[:, j, 0:1], a3[:, :, 1:64], ALU.mult, ALU.add)
        stt(a3[:, :, 0:63], u3[:, :, 1:64], cw_pm[:, j, 2:3], a3[:, :, 0:63], ALU.mult, ALU.add)

    def conv_g3(j):
        w = loadA(11 + 3 * j)
        bank = fm_group(w, lambda kc: hT[:, kc, :])
        stt(ypre[:, j, :], bank[:, :], bin_pm[:, 11 + 3 * j:12 + 3 * j], ACC, ALU.add, ALU.mult)

    ost_slots = [xst[:, :], scrb[:, 0:2048].bitcast(F32)]

    def final_norm(i, subs=range(4)):
        for s in subs:
            sc = 12 + 3 * s
            act(sq, xres[:, s, :], AF.Square)
            P.add("dve", lambda e, sc=sc: e.tensor_reduce(out=stat[:, sc:sc + 1], in_=sq, axis=AX.X, op=ALU.add),
                  [stat[:, sc:sc + 1]], [sq])
            act(stat[:, sc + 1:sc + 2], stat[:, sc:sc + 1], AF.Ln, bias=epsc[:, 0:1], scale=1.0 / D)
            act(stat[:, sc + 2:sc + 3], stat[:, sc + 1:sc + 2], AF.Exp, scale=-0.5)
            o = state["ost"]
            state["ost"] = 1 - o
            stt(ost_slots[o], xres[:, s, :], stat[:, sc + 2:sc + 3], gfin_bc[:, :], ALU.mult, ALU.mult)
            dma("sp", out_d[i * TT + s * 128:i * TT + (s + 1) * 128, :], ost_slots[o], f"out{o}", track_in=True)

    def scores(c):
        t0 = c * 128
        pf, pb = b_p[c % 2]
        bank = nb()
        for h in range(4):
            mm(bank[:, h * 128:(h + 1) * 128], kT[:, h, t0:t0 + 128], qT[:, h, t0:t0 + 128], True, True)
        tt(v4(pf), v4(bank[:, :]), maskb[:, 0:1, :].broadcast_to([128, 4, 128]), ALU.mult)
        bank = nb()
        for h in range(4):
            mm(bank[:, h * 128:(h + 1) * 128], kbB[:, c, h * 128:(h + 1) * 128],
               qbB[:, c, h * 128:(h + 1) * 128], True, True)
        tt(v4(pb), v4(bank[:, :]), maskb[:, 1:2, :].broadcast_to([128, 4, 128]), ALU.mult)

    def out_T(c):
        t0 = c * 128
        bank = nb()
        bv = bank[:, :].bitcast(BF)
        for j in range(8):
            tr(bv[:, j * 128:(j + 1) * 128], b_on[:, j * 128:(j + 1) * 128])
        copy("dve", ogT[:, :, t0:t0 + 128], bv.rearrange("p (j t) -> p j t", j=8))

    for i in range(NT):
        for h in range(4):
            w = loadA(h)
            bank = fm_group(w, lambda kc: hT[:, kc, :])
            act(qT[:, h, :], bank[:, :], AF.Identity, bias=bq_s[:, h:h + 1], scale=QSCALE)
        for h in range(4):
            w = loadA(4 + h)
            bank = fm_group(w, lambda kc: hT[:, kc, :])
            ts(kT[:, h, :], bank[:, :], bin_pm[:, 4 + h:5 + h], ALU.add)
        inproj_lr(512)
        for s in range(4):
            bank = nb()
            bv = bank[:, :].bitcast(BF)
            for h in range(4):
                tr(bv[:, h * 128:(h + 1) * 128], kT[:, h, s * 128:(s + 1) * 128])
            copy("act", ktm[:, s, :], bv[:, 0:512])
        kvu = kv_units(4, pieces=(1, 2))
        for s in range(4):
            kvu[2 * s]()
            if i > 0:
                final_norm(i - 1, [s])
            xload(xres[:, s, :], i, s, f"x{s}")
            kvu[2 * s + 1]()
        copy("dve", Sb[:, :, :].rearrange("p h v -> p (h v)"), Bnd[:, i, :])
        order = [(3, 1), (2, 1), (1, 1), (0, 1), (0, 0), (1, 0), (2, 0), (3, 0)]
        cgroups = []
        for j in range(4):
            cgroups += [(conv_g1, j), (conv_g2, j), (conv_g3, j)]
        cg = iter(cgroups)

        def next_cg():
            f, j = next(cg)
            f(j)
        tok = gla_Da(*order[0])
        for n, (c, d) in enumerate(order):
            next_cg()
            nxt = gla_Da(*order[n + 1]) if n + 1 < 8 else None
            gla_Db(tok, True)
            tok = nxt
            if 1 <= n <= 4:
                cb = 4 - n
                copy("act", Sbbf[:, cb, :], Sb[:, :, :].rearrange("p h v -> p (h v)"))
                state_update(cb, 1, Sb)
            if n % 2 == 0:
                next_cg()
        state["banks"] = [0, 1, 2, 3]
        scores(0)
        for c in range(4):
            t0 = c * 128
            pf, pb = b_p[c % 2]
            if c < 3:
                scores(c + 1)
            conv_g1(4 + c)
            ob = (ps[4 + 2 * (c % 2)], ps[5 + 2 * (c % 2)])
            for h in range(4):
                reg = ob[h // 2][:, (h % 2) * 256:(h % 2 + 1) * 256]
                vh = vtm[:, c, h * 256:(h + 1) * 256]
                mm(reg, pf[:, h * 128:(h + 1) * 128], vh, True, False)
                mm(reg, qT[:, h, t0:t0 + 128], Sfbf[:, h, :], False, False)
                mm(reg, pb[:, h * 128:(h + 1) * 128], vh, False, False)
                mm(reg, qbB[:, c, h * 128:(h + 1) * 128], Sbbf[:, c, h * 256:(h + 1) * 256], False, True)
            state_update(c, 0, Sf)
            conv_g2(4 + c)
            if c > 0:
                out_T(c - 1)
            copy("act", Sfbf[:, :, :], Sf[:, :, :])
            for hh in range(2):
                act(sq[:, hh * 512:(hh + 1) * 512], ob[hh][:, :], AF.Square)
            P.add("dve", lambda e: e.tensor_reduce(out=stat[:, 36:40], in_=sq.rearrange("p (h v) -> p h v", h=4),
                                                   axis=AX.X, op=ALU.add),
                  [stat[:, 36:40]], [sq])
            act(stat[:, 40:44], stat[:, 36:40], AF.Ln, bias=epsc[:, 0:1], scale=1.0 / 256.0)
            act(stat[:, 44:48], stat[:, 40:44], AF.Exp, scale=-0.5)
            for h in range(4):
                act(b_on[:, h * 256:(h + 1) * 256], ob[h // 2][:, (h % 2) * 256:(h % 2 + 1) * 256],
                    AF.Copy, scale=stat[:, 44 + h:45 + h])
            conv_g3(4 + c)
        out_T(3)
        state["banks"] = list(range(8))
        for j in range(8):
            w = loadA(33 + j)
            bank = fm_group(w, lambda kc: hT[:, kc, :])
            act(S(j % 2), bank[:, :], AF.Silu, bias=bin_pm[:, 33 + j:34 + j])
            stt(ogT[:, j, :], ogT[:, j, :], gg16[:, j:j + 1], S(j % 2), ALU.mult, ALU.mult)
        for j in range(8):
            o4 = 2 + 2 * (j % 2)
            w = loadA(41 + 4 * j)
            bank_a = fm_group(w, lambda kc: ypre[:, kc, :])
            w = loadA(42 + 4 * j)
            bank_b = fm_group(w, lambda kc: ogT[:, kc, :])
            w = loadA(43 + 4 * j)
            bank = fm_group(w, lambda kc: hT[:, kc, :])
            act(S(o4), bank[:, :], AF.Sigmoid, bias=bin_pm[:, 41 + 2 * j:42 + 2 * j])
            w = loadA(44 + 4 * j)
            bank = fm_group(w, lambda kc: hT[:, kc, :])
            act(S(o4 + 1), bank[:, :], AF.Sigmoid, bias=bin_pm[:, 42 + 2 * j:43 + 2 * j])
            tt(S(o4), bank_a[:, :], S(o4), ALU.mult)
            tt(S(o4 + 1), bank_b[:, :], S(o4 + 1), ALU.mult)
            tt(yT[:, j, :], S(o4), S(o4 + 1), ALU.add)
        wo = [loadB(3), loadB(4)]
        for s in range(4):
            for fh in range(2):
                bank = nb()
                for kc in range(8):
                    mm(bank[:, :], yT[:, kc, s * 128:(s + 1) * 128], wo[fh](kc), kc == 0, kc == 7)
                t = S((2 * s + fh) % 4)
                tt(t, bank[:, :], ga_bc[:, 0, fh * 512:(fh + 1) * 512], ALU.mult)
                tt(xres[:, s, fh * 512:(fh + 1) * 512], xres[:, s, fh * 512:(fh + 1) * 512], t, ALU.add)
            norm_stats(xres[:, s, :], s, 12 + 3 * s, on_act=True)
        norm_T_pipelined(2, 3, 0)
        for m in range(32):
            w = loadA(73 + m)
            bank = fm_group(w, lambda kc: hT[:, kc, :])
            act(S(m % 2), bank[:, :], AF.Relu)
            act(aT[:, m, :], S(m % 2), AF.Square)
            if i + 1 < NT and m % 8 == 7:
                s = m // 8
                xload(xst[:, :], i + 1, s, "xst")
                norm_stats(xst[:, :], s, 24 + 3 * s)
        for fh in range(2):
            banks = [nb() for _ in range(4)]
            for g in range(4):
                w = loadB(5 + fh * 4 + g)
                for kcl in range(8):
                    kc = g * 8 + kcl
                    for s in range(4):
                        mm(banks[s][:, :], aT[:, kc, s * 128:(s + 1) * 128], w(kcl), kc == 0, kc == 31)
                if i + 1 < NT:
                    norm_T(4, 0, 0, 0, js=[fh * 4 + g])
            for s in range(4):
                tt(S(2 + s % 2), banks[s][:, :], ga_bc[:, 1, fh * 512:(fh + 1) * 512], ALU.mult)
                tt(xres[:, s, fh * 512:(fh + 1) * 512], xres[:, s, fh * 512:(fh + 1) * 512], S(2 + s % 2), ALU.add)

    final_norm(NT - 1)

    P.emit(final_waits=["out0", "out1"])
    es.close()
    return nc


def _pm(v):
    return np.ascontiguousarray(v.reshape(-1, 128).T)


def _panel(w, c0, ncol, width=None):
    K = w.shape[0]
    width = width or ncol
    out = np.zeros((128, K // 128, width), np.float32)
    out[:, :, :ncol] = w[:, c0:c0 + ncol].reshape(K // 128, 128, ncol).transpose(1, 0, 2)
    return out.reshape(128, -1)


_NC_CACHE = {}


def kernel(x, c, ctx, c_ctx, w_ada, b_ada, g_norm1, w_in, b_in, conv_w, w_conv_out, w_a2_f, b_a_f,
           w_a2_b, b_a_b, g_gla_norm, w_gla_out, w_o, g_norm2, w_up, w_down, g_final):
    f = np.float32
    x = np.asarray(x, f); c = np.asarray(c, f); ctx = np.asarray(ctx, f); c_ctx = np.asarray(c_ctx, f)
    w_ada = np.asarray(w_ada, f)[0]; b_ada = np.asarray(b_ada, f)[0]
    g1 = np.asarray(g_norm1, f)[0]; w_in = np.asarray(w_in, f)[0]; b_in = np.asarray(b_in, f)[0]
    conv_w = np.asarray(conv_w, f)[0]; w_co = np.asarray(w_conv_out, f)[0]
    waf = np.asarray(w_a2_f, f)[0]; baf = np.asarray(b_a_f, f)[0]
    wab = np.asarray(w_a2_b, f)[0]; bab = np.asarray(b_a_b, f)[0]
    gg = np.asarray(g_gla_norm, f)[0]; w_go = np.asarray(w_gla_out, f)[0]; w_o = np.asarray(w_o, f)[0]
    g2 = np.asarray(g_norm2, f)[0]; w_up = np.asarray(w_up, f)[0]; w_down = np.asarray(w_down, f)[0]
    g_final = np.asarray(g_final, f)

    XA, BA, CA, Q, K_, V, R, LRF, LRB, GA, GB = 0, 1024, 2048, 3072, 3584, 4096, 5120, 6144, 6160, 6176, 7200

    fm = []
    fm_bias = []
    for h in range(4):
        fm.append(_panel(w_in, Q + h * 128, 128)); fm_bias.append(b_in[Q + h * 128:Q + (h + 1) * 128])
    for h in range(4):
        fm.append(_panel(w_in, K_ + h * 128, 128)); fm_bias.append(b_in[K_ + h * 128:K_ + (h + 1) * 128])
    lrw = np.zeros((1024, 128), f)
    lrw[:, 0:16] = w_in[:, LRF:LRF + 16]
    lrw[:, 32:48] = w_in[:, LRB:LRB + 16]
    lrb = np.zeros(128, f)
    lrb[0:16] = b_in[LRF:LRF + 16]; lrb[16] = 1.0
    lrb[32:48] = b_in[LRB:LRB + 16]; lrb[48] = 1.0
    fm.append(_panel(lrw, 0, 128)); fm_bias.append(lrb)
    for j in range(8):
        for base in (XA, CA, BA):
            fm.append(_panel(w_in, base + j * 128, 128)); fm_bias.append(b_in[base + j * 128:base + (j + 1) * 128])
    for j in range(8):
        fm.append(_panel(w_in, R + j * 128, 128)); fm_bias.append(b_in[R + j * 128:R + (j + 1) * 128])
    gate_bias = []
    for j in range(8):
        fm.append(_panel(w_co, j * 128, 128))
        fm.append(_panel(w_go, j * 128, 128))
        fm.append(_panel(w_in, GA + j * 128, 128)); gate_bias.append(b_in[GA + j * 128:GA + (j + 1) * 128])
        fm.append(_panel(w_in, GB + j * 128, 128)); gate_bias.append(b_in[GB + j * 128:GB + (j + 1) * 128])
    for m in range(32):
        fm.append(_panel(w_up, m * 128, 128))
    wfm = np.stack(fm)
    assert wfm.shape == (NFM, 128, 1024)
    bin_pm = np.stack(fm_bias + gate_bias, axis=1)
    assert bin_pm.shape == (128, 57)

    tm = [_panel(w_in, K_, 512), _panel(w_in, V, 512), _panel(w_in, V + 512, 512),
          _panel(w_o, 0, 512), _panel(w_o, 512, 512)]
    for fh in range(2):
        for g in range(4):
            tm.append(_panel(w_down[g * 1024:(g + 1) * 1024], fh * 512, 512))
    wtm = np.stack(tm)
    assert wtm.shape == (NTM, 128, 4096)
    binkv = np.concatenate([b_in[K_:K_ + 512], b_in[V:V + 1024]])[None, :]

    wada = np.stack([_panel(w_ada, j * 512, 512) for j in range(12)])
    smallpm = np.concatenate([
        _pm(b_ada), bin_pm, _pm(g1), _pm(g2), _pm(gg),
        np.stack([_pm(conv_w[i]) for i in range(3)], axis=2).reshape(128, 24),
    ], axis=1).astype(f)
    assert smallpm.shape == (128, 153)

    wa2 = np.zeros((2, 128, 512), f)
    wa2[0, 0:16] = waf; wa2[0, 16] = baf
    wa2[1, 32:48] = wab; wa2[1, 48] = bab

    gfin = np.ascontiguousarray(np.broadcast_to(g_final[None, :], (128, 1024))).astype(f)

    s_i = np.arange(128)[:, None]
    t_i = np.arange(128)[None, :]
    ng = -1.0 / 16.0
    cst = np.stack([
        (s_i <= t_i) * ng, (s_i >= t_i) * ng, (s_i > t_i) * ng, (s_i < t_i) * ng,
        (s_i <= t_i) * 1.0, (s_i >= t_i) * 1.0, (s_i == t_i) * 1.0, np.ones((128, 128)),
    ], axis=1).astype(f).reshape(128, 8 * 128)

    shared = dict(wada=wada, badarow=b_ada[None, :].copy(), smallpm=smallpm, binkv=binkv.astype(f),
                  wfm=wfm, wtm=wtm, wa2=wa2, gfin=gfin, cst=cst)
    in_maps = []
    for b in range(8):
        cc = np.stack([_pm(c[b]), _pm(c_ctx)], axis=2).reshape(128, 16).astype(f)
        m = dict(shared)
        m.update(x=np.ascontiguousarray(x[b]), ctx=np.ascontiguousarray(ctx[b]), cc=cc)
        in_maps.append(m)

    if "nc" not in _NC_CACHE:
        _NC_CACHE["nc"] = build_nc()
    res = run_bass_kernel_spmd(_NC_CACHE["nc"], in_maps, core_ids=list(range(8)))
    return np.stack([np.asarray(r["out"], dtype=np.float32) for r in res.results], axis=0)
```
